# Optimizing a Trainium2 kernel written in Bass

```python
import math
import jax, jax.numpy as jnp
from jax import lax
import numpy as np

D_MODEL = 1024
BATCH = 8
SEQ = 2048
DEPTH = 4

CHUNK = 64
QBLK = 128
DSA_QBLK = 64
EPS = 1e-6

LRU_WIDTH = D_MODEL // 2
LRU_BLOCKS = 8
LRU_BW = LRU_WIDTH // LRU_BLOCKS
CONV_W = 4
LRU_C = 8.0

MLA_HEADS = 8
MLA_NOPE = 64
MLA_ROPE = 32
MLA_V = 64
MLA_Q_RANK = D_MODEL // 4
MLA_KV_RANK = D_MODEL // 8
ROPE_THETA = 10000.0

EVEN_SPLITS = (LRU_WIDTH, LRU_WIDTH, MLA_Q_RANK, MLA_KV_RANK, MLA_ROPE)
EVEN_IN = sum(EVEN_SPLITS)
EVEN_MIX = LRU_WIDTH + MLA_HEADS * MLA_V

DSA_HEADS = 16
DSA_KV_HEADS = 4
DSA_HD = 64
IDX_HEADS = 8
IDX_HD = 64
TOPK_MAX = 256
ODD_SPLITS = (DSA_HEADS * DSA_HD, DSA_KV_HEADS * DSA_HD, DSA_KV_HEADS * DSA_HD,
              IDX_HEADS * IDX_HD, IDX_HD, IDX_HEADS)
ODD_IN = sum(ODD_SPLITS)
ODD_MIX = DSA_HEADS * DSA_HD

D_FF = 4 * D_MODEL

N_EVEN = (DEPTH + 1) // 2
N_ODD = DEPTH // 2

kernel_name = "hybrid_chunk_causal_rglru_mla_dsa"


def rms_norm(x, g):
    xf = x.astype(jnp.float32)
    y = xf * lax.rsqrt(jnp.mean(xf * xf, axis=-1, keepdims=True) + EPS)
    return (y * g.astype(jnp.float32)).astype(x.dtype)


def split_cols(u, sizes):
    offs = [int(o) for o in np.cumsum(sizes)[:-1]]
    return jnp.split(u, offs, axis=-1)


def rope(x, cos, sin):
    half = x.shape[-1] // 2
    x1, x2 = x[..., :half], x[..., half:]
    return jnp.concatenate([x1 * cos - x2 * sin, x1 * sin + x2 * cos], axis=-1)


def causal_conv(x, w, b):
    S = x.shape[1]
    xp = jnp.pad(x, ((0, 0), (CONV_W - 1, 0), (0, 0)))
    return sum(xp[:, j:j + S] * w[j] for j in range(CONV_W)) + b


def rg_lru(x, ga_w, ga_b, gx_w, gx_b, lam):
    B, S, W = x.shape
    xf = x.astype(jnp.float32)
    xb = xf.reshape(B, S, LRU_BLOCKS, LRU_BW)
    r = jax.nn.sigmoid(jnp.einsum('bsnk,nkj->bsnj', xb, ga_w.astype(jnp.float32)).reshape(B, S, W)
                       + ga_b.astype(jnp.float32))
    i = jax.nn.sigmoid(jnp.einsum('bsnk,nkj->bsnj', xb, gx_w.astype(jnp.float32)).reshape(B, S, W)
                       + gx_b.astype(jnp.float32))
    log_a = -LRU_C * r * jax.nn.softplus(-lam.astype(jnp.float32))
    a = jnp.exp(log_a)
    mult = jnp.sqrt(-jnp.expm1(2.0 * log_a))
    bterm = mult * (i * xf)

    def combine(left, right):
        a_l, b_l = left
        a_r, b_r = right
        return a_l * a_r, a_r * b_l + b_r

    _, h = lax.associative_scan(combine, (a, bterm), axis=1)
    return h.astype(x.dtype)


def mla_attention(q_nope, q_rope, k_nope, k_rope, v):
    S = q_nope.shape[1]
    scale = (MLA_NOPE + MLA_ROPE) ** -0.5
    outs = []
    for blk in range(S // QBLK):
        q0, q1 = blk * QBLK, (blk + 1) * QBLK
        s = (jnp.einsum('bqhd,bkhd->bhqk', q_nope[:, q0:q1], k_nope[:, :q1])
             + jnp.einsum('bqhr,bkr->bhqk', q_rope[:, q0:q1], k_rope[:, :q1]))
        s = s.astype(jnp.float32) * scale
        qi = jnp.arange(q0, q1) // CHUNK
        ki = jnp.arange(q1) // CHUNK
        mask = ki[None, :] <= qi[:, None]
        s = jnp.where(mask, s, -jnp.inf)
        p = jax.nn.softmax(s, axis=-1).astype(v.dtype)
        outs.append(jnp.einsum('bhqk,bkhd->bqhd', p, v[:, :q1]))
    return jnp.concatenate(outs, axis=1)


def dsa_attention(q, k, v, q_idx, k_idx, w_idx):
    B, S = q.shape[0], q.shape[1]
    topk = min(TOPK_MAX, S // 4)
    rep = DSA_HEADS // DSA_KV_HEADS
    scale = DSA_HD ** -0.5
    idx_scale = IDX_HD ** -0.5
    bidx = jnp.arange(B)[:, None, None]
    outs = []
    for blk in range(S // DSA_QBLK):
        q0, q1 = blk * DSA_QBLK, (blk + 1) * DSA_QBLK
        kl = min(S, max(q1, topk))
        qi = jnp.arange(q0, q1) // CHUNK
        ki = jnp.arange(kl) // CHUNK
        adm = ki[None, :] <= qi[:, None]
        logits = jnp.einsum('bqhd,bkd->bqhk', q_idx[:, q0:q1], k_idx[:, :kl]).astype(jnp.float32) * idx_scale
        score = jnp.einsum('bqh,bqhk->bqk', w_idx[:, q0:q1].astype(jnp.float32), jax.nn.relu(logits))
        score = jnp.where(adm[None], score, -jnp.inf)
        _, sel = lax.top_k(score, topk)
        valid = (sel // CHUNK) <= qi[None, :, None]
        ks = k[bidx, sel]
        vs = v[bidx, sel]
        qb = q[:, q0:q1].reshape(B, DSA_QBLK, DSA_KV_HEADS, rep, DSA_HD)
        s = jnp.einsum('bqgrd,bqkgd->bgrqk', qb, ks).astype(jnp.float32) * scale
        s = jnp.where(valid[:, None, None], s, -jnp.inf)
        p = jax.nn.softmax(s, axis=-1).astype(v.dtype)
        o = jnp.einsum('bgrqk,bqkgd->bqgrd', p, vs)
        outs.append(o.reshape(B, DSA_QBLK, DSA_HEADS * DSA_HD))
    return jnp.concatenate(outs, axis=1)


def even_mixer(x, cos, sin, norm_g, w_in, conv_w, conv_b, ga_w, ga_b, gx_w, gx_b, lam,
               q_norm, w_uq, kv_norm, w_ukv, w_out):
    B, S, _ = x.shape
    h = rms_norm(x, norm_g)
    u = h @ w_in
    xr, gr, cq, ckv, kr = split_cols(u, EVEN_SPLITS)
    hr = rg_lru(causal_conv(xr, conv_w, conv_b), ga_w, ga_b, gx_w, gx_b, lam)
    y_lru = hr * jax.nn.gelu(gr)
    q = (rms_norm(cq, q_norm) @ w_uq).reshape(B, S, MLA_HEADS, MLA_NOPE + MLA_ROPE)
    q_nope, q_rope = q[..., :MLA_NOPE], q[..., MLA_NOPE:]
    kv = (rms_norm(ckv, kv_norm) @ w_ukv).reshape(B, S, MLA_HEADS, MLA_NOPE + MLA_V)
    k_nope, v = kv[..., :MLA_NOPE], kv[..., MLA_NOPE:]
    q_rope = rope(q_rope, cos[:, :, None, :], sin[:, :, None, :])
    k_rope = rope(kr, cos, sin)
    y_mla = mla_attention(q_nope, q_rope, k_nope, k_rope, v).reshape(B, S, MLA_HEADS * MLA_V)
    return jnp.concatenate([y_lru, y_mla], axis=-1) @ w_out


def odd_mixer(x, norm_g, w_in, idx_k_norm, w_out):
    B, S, _ = x.shape
    h = rms_norm(x, norm_g)
    u = h @ w_in
    q, k, v, qi, ki, wi = split_cols(u, ODD_SPLITS)
    q = q.reshape(B, S, DSA_HEADS, DSA_HD)
    k = k.reshape(B, S, DSA_KV_HEADS, DSA_HD)
    v = v.reshape(B, S, DSA_KV_HEADS, DSA_HD)
    qi = qi.reshape(B, S, IDX_HEADS, IDX_HD)
    ki = rms_norm(ki, idx_k_norm)
    wi = wi * (IDX_HEADS ** -0.5)
    return dsa_attention(q, k, v, qi, ki, wi) @ w_out


def mlp(x, norm_g, w1, w2):
    h = rms_norm(x, norm_g)
    return jnp.square(jax.nn.relu(h @ w1)) @ w2


def setup_inputs(seed: int = 0) -> dict:
    key = jax.random.key(seed)
    ks = iter(jax.random.split(key, 32))
    f32 = jnp.float32

    def nrm(shape, fan_in):
        return jax.random.normal(next(ks), shape, f32) * (fan_in ** -0.5)

    def gain(shape):
        return 1.0 + 0.02 * jax.random.normal(next(ks), shape, f32)

    def bias(shape):
        return 0.02 * jax.random.normal(next(ks), shape, f32)

    x = jax.random.normal(next(ks), (BATCH, SEQ, D_MODEL), f32)
    positions = jnp.broadcast_to(jnp.arange(SEQ, dtype=jnp.int32), (BATCH, SEQ))
    a0 = jax.random.uniform(next(ks), (N_EVEN, LRU_WIDTH), f32, 0.9, 0.999)
    p = a0 ** (1.0 / LRU_C)
    lam = jnp.log(p) - jnp.log1p(-p)
    return {
        "x": x,
        "positions": positions,
        "e_norm": gain((N_EVEN, D_MODEL)),
        "e_w_in": nrm((N_EVEN, D_MODEL, EVEN_IN), D_MODEL),
        "e_conv_w": nrm((N_EVEN, CONV_W, LRU_WIDTH), CONV_W),
        "e_conv_b": bias((N_EVEN, LRU_WIDTH)),
        "e_ga_w": nrm((N_EVEN, LRU_BLOCKS, LRU_BW, LRU_BW), LRU_BW),
        "e_ga_b": bias((N_EVEN, LRU_WIDTH)),
        "e_gx_w": nrm((N_EVEN, LRU_BLOCKS, LRU_BW, LRU_BW), LRU_BW),
        "e_gx_b": bias((N_EVEN, LRU_WIDTH)),
        "e_lambda": lam,
        "e_q_norm": gain((N_EVEN, MLA_Q_RANK)),
        "e_w_uq": nrm((N_EVEN, MLA_Q_RANK, MLA_HEADS * (MLA_NOPE + MLA_ROPE)), MLA_Q_RANK),
        "e_kv_norm": gain((N_EVEN, MLA_KV_RANK)),
        "e_w_ukv": nrm((N_EVEN, MLA_KV_RANK, MLA_HEADS * (MLA_NOPE + MLA_V)), MLA_KV_RANK),
        "e_w_out": nrm((N_EVEN, EVEN_MIX, D_MODEL), EVEN_MIX),
        "o_norm": gain((N_ODD, D_MODEL)),
        "o_w_in": nrm((N_ODD, D_MODEL, ODD_IN), D_MODEL),
        "o_idx_k_norm": gain((N_ODD, IDX_HD)),
        "o_w_out": nrm((N_ODD, ODD_MIX, D_MODEL), ODD_MIX),
        "m_norm": gain((DEPTH, D_MODEL)),
        "m_w1": nrm((DEPTH, D_MODEL, D_FF), D_MODEL),
        "m_w2": nrm((DEPTH, D_FF, D_MODEL), D_FF),
        "final_norm": gain((D_MODEL,)),
    }


def reference(x, positions, e_norm, e_w_in, e_conv_w, e_conv_b, e_ga_w, e_ga_b, e_gx_w, e_gx_b,
              e_lambda, e_q_norm, e_w_uq, e_kv_norm, e_w_ukv, e_w_out, o_norm, o_w_in,
              o_idx_k_norm, o_w_out, m_norm, m_w1, m_w2, final_norm):
    freqs = ROPE_THETA ** (-jnp.arange(0, MLA_ROPE, 2, dtype=jnp.float32) / MLA_ROPE)
    ang = positions.astype(jnp.float32)[..., None] * freqs
    cos = jnp.cos(ang).astype(x.dtype)
    sin = jnp.sin(ang).astype(x.dtype)
    for l in range(DEPTH):
        j = l // 2
        if l % 2 == 0:
            x = x + even_mixer(x, cos, sin, e_norm[j], e_w_in[j], e_conv_w[j], e_conv_b[j],
                               e_ga_w[j], e_ga_b[j], e_gx_w[j], e_gx_b[j], e_lambda[j],
                               e_q_norm[j], e_w_uq[j], e_kv_norm[j], e_w_ukv[j], e_w_out[j])
        else:
            x = x + odd_mixer(x, o_norm[j], o_w_in[j], o_idx_k_norm[j], o_w_out[j])
        x = x + mlp(x, m_norm[l], m_w1[l], m_w2[l])
    return rms_norm(x, final_norm)
```

```python
import os
from contextlib import ExitStack
import numpy as np
import concourse.bass as bass
import concourse.mybir as mybir
from concourse.bass_utils import run_bass_kernel_spmd

F32 = mybir.dt.float32
BF16 = mybir.dt.bfloat16
I32 = mybir.dt.int32
AF = mybir.ActivationFunctionType
ALU = mybir.AluOpType
AX = mybir.AxisListType

S = 2048
D = 1024
NG = 4
GS = 512
EPS = 1e-6
NEG = -30000.0
TOPK = 256
NBIS = 16


class Tok:
    __slots__ = ("w", "r")

    def __init__(self, fence):
        self.w = None
        self.r = dict(fence)


class MK:
    ENG = ["pe", "act", "dve", "pool", "sp"]
    NDMA = 24

    def __init__(self, nc):
        self.nc = nc
        self.prog = {e: [] for e in self.ENG}
        self.cnt = {e: 0 for e in self.ENG}
        self.known = {e: {} for e in self.ENG}
        self.dma_n = 0
        self.dma_last = [0] * self.NDMA
        self.stacks = [ExitStack()]
        self.scope_toks = [[]]
        self.fence = {}
        self.nbank = 0

    def sb(self, name, shape, dt):
        self.nname = getattr(self, "nname", 0) + 1
        return self.stacks[-1].enter_context(self.nc.sbuf_tensor("%s_%d" % (name, self.nname), list(shape), dt))

    def ps(self, name, shape, dt=F32):
        self.nname = getattr(self, "nname", 0) + 1
        return self.stacks[-1].enter_context(self.nc.psum_tensor("%s_%d" % (name, self.nname), list(shape), dt))

    def tok(self):
        t = Tok(self.fence)
        self.scope_toks[-1].append(t)
        return t

    def toks(self, n):
        return [self.tok() for _ in range(n)]

    def open(self):
        self.stacks.append(ExitStack())
        self.scope_toks.append([])

    def close(self):
        for t in self.scope_toks.pop():
            if t.w is not None:
                k, v = t.w
                if self.fence.get(k, 0) < v:
                    self.fence[k] = v
            for k, v in t.r.items():
                if self.fence.get(k, 0) < v:
                    self.fence[k] = v
        self.stacks.pop().close()

    def _deps(self, eng, reads, writes):
        deps = {}

        def add(k, v):
            if deps.get(k, 0) < v:
                deps[k] = v

        for t in reads:
            if t.w is not None:
                add(*t.w)
        for t in writes:
            if t.w is not None:
                add(*t.w)
            for k, v in t.r.items():
                if k == eng:
                    continue
                add(k, v)
        waits = []
        for k, v in deps.items():
            if k == eng and eng == "pe":
                continue
            if self.known[eng].get(k, 0) >= v:
                continue
            self.known[eng][k] = v
            waits.append((k, v))
        return waits

    def op(self, eng, fn, reads=(), writes=()):
        waits = self._deps(eng, reads, writes)
        self.cnt[eng] += 1
        mark = (eng, self.cnt[eng])
        self.prog[eng].append((waits, fn, mark))
        for t in reads:
            t.r[eng] = mark[1]
        for t in writes:
            t.w = mark
            t.r = {}
        return mark

    def dma(self, q, out, in_, reads=(), writes=(), **kw):
        idx = self.dma_n % self.NDMA
        self.dma_n += 1
        key = ("dma", idx)
        waits = self._deps(q, reads, writes)
        prev = self.dma_last[idx]
        if prev and self.known[q].get(key, 0) < prev:
            self.known[q][key] = prev
            waits.append((key, prev))
        val = prev + 16
        self.dma_last[idx] = val
        mark = (key, val)

        def fn(e, out=out, in_=in_, kw=kw):
            return e.dma_start(out=out, in_=in_, **kw)

        self.prog[q].append((waits, fn, mark))
        for t in reads:
            t.r[key] = val
        for t in writes:
            t.w = mark
            t.r = {}
        return mark

    def finish(self, toks):
        waits = [t.w for t in toks if t.w is not None]
        self.prog["sp"].append((waits, None, None))

    def emit(self):
        nc = self.nc
        es = self.stacks[0]
        sems = {}
        for e in self.ENG:
            sems[e] = es.enter_context(nc.semaphore("s_" + e))
        for i in range(self.NDMA):
            sems[("dma", i)] = es.enter_context(nc.semaphore("s_dma%d" % i))
        prog = self.prog

        def run(engname, e):
            for waits, fn, mark in prog[engname]:
                for k, v in waits:
                    e.wait_ge(sems[k], v)
                if fn is None:
                    continue
                inst = fn(e)
                if mark[0] == engname:
                    inst.then_inc(sems[engname], 1)
                else:
                    inst.then_inc(sems[mark[0]], 16)

        with nc.Block() as block:
            @block.tensor
            def _(e):
                run("pe", e)

            @block.scalar
            def _(e):
                run("act", e)

            @block.vector
            def _(e):
                run("dve", e)

            @block.gpsimd
            def _(e):
                run("pool", e)

            @block.sync
            def _(e):
                run("sp", e)
        es.close()


Q_A = [0, 1, 2, 3, 8, 9, 10, 11]
Q_B = [4, 5, 6, 7, 12, 13, 14, 15]

VEC = {}


def _vec_layout():
    c = 0
    for j in range(2):
        for nm, n in (("e_norm", 8), ("conv_w", 16), ("conv_b", 4), ("ga_b", 4), ("gx_b", 4), ("lam", 4),
                      ("q_norm", 2), ("kv_norm", 1)):
            VEC[(nm, j)] = c
            c += n
    for j in range(2):
        VEC[("o_norm", j)] = c
        c += 8
        VEC[("ikn", j)] = c
        c += 1
    for l in range(4):
        VEC[("m_norm", l)] = c
        c += 8
    VEC["final"] = c
    c += 8
    VEC["freq"] = c
    c += 1
    VEC["sgn"] = c
    c += 1
    return c


NVEC = _vec_layout()


def _cols(v):
    v = np.asarray(v, np.float32)
    return v.reshape(-1, 128).T


def prep_shared(inp):
    f = np.float32
    sh = {}
    vec = np.zeros((128, NVEC), f)
    for j in range(2):
        vec[:, VEC[("e_norm", j)]:VEC[("e_norm", j)] + 8] = _cols(inp["e_norm"][j])
        cw = inp["e_conv_w"][j]
        for t in range(4):
            vec[:, VEC[("conv_w", j)] + 4 * t:VEC[("conv_w", j)] + 4 * t + 4] = _cols(cw[t])
        vec[:, VEC[("conv_b", j)]:VEC[("conv_b", j)] + 4] = _cols(inp["e_conv_b"][j])
        vec[:, VEC[("ga_b", j)]:VEC[("ga_b", j)] + 4] = _cols(inp["e_ga_b"][j])
        vec[:, VEC[("gx_b", j)]:VEC[("gx_b", j)] + 4] = _cols(inp["e_gx_b"][j])
        vec[:, VEC[("lam", j)]:VEC[("lam", j)] + 4] = _cols(inp["e_lambda"][j])
        vec[:, VEC[("q_norm", j)]:VEC[("q_norm", j)] + 2] = _cols(inp["e_q_norm"][j])
        vec[:, VEC[("kv_norm", j)]:VEC[("kv_norm", j)] + 1] = _cols(inp["e_kv_norm"][j])
        vec[:, VEC[("o_norm", j)]:VEC[("o_norm", j)] + 8] = _cols(inp["o_norm"][j])
        g = np.asarray(inp["o_idx_k_norm"][j], f)
        vec[:, VEC[("ikn", j)]] = np.concatenate([g, g])
    for l in range(4):
        vec[:, VEC[("m_norm", l)]:VEC[("m_norm", l)] + 8] = _cols(inp["m_norm"][l])
    vec[:, VEC["final"]:VEC["final"] + 8] = _cols(inp["final_norm"])
    freqs = (10000.0 ** (-np.arange(0, 32, 2, dtype=np.float32) / 32)).astype(f)
    vec[64:96, VEC["freq"]] = np.concatenate([freqs, freqs])
    vec[64:80, VEC["sgn"]] = -1.0
    vec[80:96, VEC["sgn"]] = 1.0
    sh["vecs"] = vec

    e_w_in = np.asarray(inp["e_w_in"], f)
    sh["e_w_in"] = np.ascontiguousarray(e_w_in)
    kr = e_w_in[:, :, 1408:1440]
    junk = e_w_in[:, :, 1344:1408]
    sh["e_w_kr"] = np.ascontiguousarray(np.concatenate(
        [junk, kr, junk, kr[:, :, 16:32], kr[:, :, 0:16]], axis=2))
    wuq = np.asarray(inp["e_w_uq"], f).reshape(2, 256, 8, 96)
    sw = np.concatenate([wuq[..., :64], wuq[..., 80:96], wuq[..., 64:80]], axis=-1)
    sh["e_w_uq"] = np.ascontiguousarray(wuq.reshape(2, 256, 768))
    sh["e_w_uq_sw"] = np.ascontiguousarray(sw.reshape(2, 256, 768))
    wukv = np.asarray(inp["e_w_ukv"], f).reshape(2, 128, 8, 128)
    sh["e_w_uk"] = np.ascontiguousarray(wukv[..., :64].reshape(2, 128, 512))
    sh["e_w_uv"] = np.ascontiguousarray(wukv[..., 64:].reshape(2, 128, 512))
    for nm, src in (("e_ga_bd", "e_ga_w"), ("e_gx_bd", "e_gx_w")):
        w = np.asarray(inp[src], f)
        bd = np.zeros((2, 4, 128, 128), f)
        for c in range(4):
            bd[:, c, 0:64, 0:64] = w[:, 2 * c]
            bd[:, c, 64:128, 64:128] = w[:, 2 * c + 1]
        sh[nm] = bd
    sh["e_w_out"] = np.ascontiguousarray(np.asarray(inp["e_w_out"], f))

    o_w_in = np.asarray(inp["o_w_in"], f)
    qcols = []
    for j in range(8):
        qcols += list(range(Q_A[j] * 64, Q_A[j] * 64 + 64)) + list(range(Q_B[j] * 64, Q_B[j] * 64 + 64))
    ki = o_w_in[:, :, 2048:2112]
    sh["o_w_in"] = np.ascontiguousarray(np.concatenate(
        [o_w_in[:, :, qcols], o_w_in[:, :, 1024:2048], ki, ki, o_w_in[:, :, 2112:2120]], axis=2))
    sh["o_w_out"] = np.ascontiguousarray(np.asarray(inp["o_w_out"], f)[:, qcols, :])
    sh["m_w1"] = np.ascontiguousarray(np.asarray(inp["m_w1"], f))
    sh["m_w2"] = np.ascontiguousarray(np.asarray(inp["m_w2"], f))

    sh["c_ident"] = np.eye(128, dtype=f)
    mm = np.zeros((4, 128, 512), f)
    kk = np.arange(128)[:, None]
    qq = np.arange(512)[None, :]
    for j in range(4):
        mm[j] = np.where((128 * j + kk) // 64 <= qq // 64, 0.0, NEG)
    sh["c_maskM"] = mm
    cm = np.zeros((128, 128), f)
    cm[0:64, 64:128] = -1e30
    sh["c_cm"] = cm
    bdo = np.zeros((128, 128), f)
    bdo[0:64, 0:64] = 1.0
    bdo[64:128, 64:128] = 1.0
    sh["c_bd1"] = bdo
    sh["c_pow"] = np.tile((2.0 ** -(np.arange(NBIS, dtype=np.float32) + 1.0))[None, :], (128, 1)).astype(f)
    return sh


SHAPES = {
    "vecs": [128, NVEC], "e_w_in": [2, 1024, 1440], "e_w_kr": [2, 1024, 192], "e_w_uq": [2, 256, 768],
    "e_w_uq_sw": [2, 256, 768], "e_w_uk": [2, 128, 512], "e_w_uv": [2, 128, 512], "e_ga_bd": [2, 4, 128, 128],
    "e_gx_bd": [2, 4, 128, 128], "e_w_out": [2, 1024, 1024], "o_w_in": [2, 1024, 2184], "o_w_out": [2, 1024, 1024],
    "m_w1": [4, 1024, 4096], "m_w2": [4, 4096, 1024], "c_ident": [128, 128], "c_maskM": [4, 128, 512],
    "c_cm": [128, 128], "c_bd1": [128, 128], "c_pow": [128, NBIS],
}


def build(stages):
    nc = bass.Bass("TRN2", target_bir_lowering=False)
    mk = MK(nc)
    small = not any(st[0] == "m" for st in stages)
    dr = {k: nc.dram_tensor(k, ([1, 128, 128] if (small and k in ("m_w1", "m_w2")) else v), F32, kind="ExternalInput").ap() for k, v in SHAPES.items()}
    xin = nc.dram_tensor("xT", [D, S], F32, kind="ExternalInput").ap()
    posin = nc.dram_tensor("pos", [1, S], I32, kind="ExternalInput").ap()
    yout = nc.dram_tensor("yT", [D, S], F32, kind="ExternalOutput").ap()

    xT = mk.sb("xT_sb", [128, 8, S], F32)
    tx = [[mk.tok() for _ in range(NG)] for _ in range(8)]
    vecs = mk.sb("vecs_sb", [128, NVEC], F32)
    tvec = mk.tok()
    ident_b = mk.sb("ident_b", [128, 128], BF16)
    ident_f = mk.sb("ident_f", [128, 128], F32)
    ones_b = mk.sb("ones_b", [128, 128], BF16)
    bd1_b = mk.sb("bd1_b", [128, 128], BF16)
    ones_f = mk.sb("ones_f", [128, 64], F32)
    maskM = mk.sb("maskM", [128, 4, 512], BF16)
    cm_f = mk.sb("cm_f", [128, 128], F32)
    pow_f = mk.sb("pow_f", [128, NBIS], F32)
    tconst = mk.tok()
    cosT = mk.sb("cosT", [96, S], F32)
    sinT = mk.sb("sinT", [96, S], F32)
    ttab = mk.tok()

    NB = 5
    banks = [mk.ps("bank%d" % i, [128, 512], F32) for i in range(NB)]
    tbank = mk.toks(NB)
    obank = [mk.ps("obank%d" % i, [128, 512], F32) for i in range(2)]
    tobank = mk.toks(2)
    state = {"b": 0, "o": 0}

    def bank():
        i = state["b"] % NB
        state["b"] += 1
        return banks[i], tbank[i]

    def obk():
        i = state["o"] % 2
        state["o"] += 1
        return obank[i], tobank[i]

    def gs(g):
        return slice(g * GS, (g + 1) * GS)

    def V(key, n=1, off=0):
        c = (VEC[key] if not isinstance(key, int) else key) + off
        return vecs[:, c:c + n]

    for c in range(8):
        mk.dma("sp", xT[:, c, :], xin[c * 128:(c + 1) * 128, :], writes=tx[c])
    mk.dma("sp", vecs[:], dr["vecs"], writes=[tvec])
    mk.dma("sp", ident_f[:], dr["c_ident"], writes=[tconst])
    mk.dma("sp", cm_f[:], dr["c_cm"], writes=[tconst])
    mk.dma("sp", pow_f[:], dr["c_pow"], writes=[tconst])
    mk.dma("pool", ident_b[:], dr["c_ident"], writes=[tconst])
    mk.dma("pool", bd1_b[:], dr["c_bd1"], writes=[tconst])
    mk.dma("pool", maskM[:], dr["c_maskM"].rearrange("j p q -> p j q"), writes=[tconst])
    mk.op("dve", lambda e: e.memset(ones_b[:], 1.0), writes=[tconst])
    mk.op("dve", lambda e: e.memset(ones_f[:], 1.0), writes=[tconst])

    def rope_tables():
        mk.open()
        R = slice(64, 96)
        pos_i = mk.sb("pos_i", [96, S], I32)
        ang = mk.sb("ang", [96, S], F32)
        kf = mk.sb("kf", [96, S], F32)
        ki_ = mk.sb("ki_", [96, S], I32)
        r2 = mk.sb("r2", [96, S], F32)
        acc = mk.sb("acc", [96, S], F32)
        t = mk.tok()
        mk.dma("sp", pos_i[R, :], posin.broadcast_to([32, S]), writes=[t])
        mk.op("dve", lambda e: e.tensor_copy(out=ang[R, :], in_=pos_i[R, :]), reads=[t], writes=[t])
        mk.op("dve", lambda e: e.tensor_scalar(out=ang[R, :], in0=ang[R, :], scalar1=V("freq")[R, :], scalar2=None, op0=ALU.mult),
              reads=[t, tvec], writes=[t])
        TWO_PI = 2.0 * np.pi
        C1 = 6.28125
        C2 = float(np.float32(TWO_PI - 6.28125))
        C3 = float(TWO_PI - 6.28125 - np.float64(np.float32(TWO_PI - 6.28125)))
        for which in range(2):
            src = ang
            if which == 1:
                mk.op("dve", lambda e: e.tensor_scalar(out=r2[R, :], in0=ang[R, :], scalar1=float(np.pi / 2), scalar2=None, op0=ALU.add),
                      reads=[t], writes=[t])
                src = r2
            mk.op("dve", lambda e, src=src: e.tensor_scalar(out=kf[R, :], in0=src[R, :], scalar1=float(1.0 / TWO_PI), scalar2=None, op0=ALU.mult),
                  reads=[t], writes=[t])
            mk.op("dve", lambda e: e.tensor_copy(out=ki_[R, :], in_=kf[R, :]), reads=[t], writes=[t])
            mk.op("dve", lambda e: e.tensor_copy(out=kf[R, :], in_=ki_[R, :]), reads=[t], writes=[t])
            for ci, cc in enumerate((C1, C2, C3)):
                s_in = src if ci == 0 else acc
                mk.op("dve", lambda e, s_in=s_in, cc=cc: e.scalar_tensor_tensor(out=acc[R, :], in0=kf[R, :], scalar=float(-cc), in1=s_in[R, :],
                                                                              op0=ALU.mult, op1=ALU.add), reads=[t], writes=[t])
            mk.op("dve", lambda e: e.tensor_scalar(out=kf[R, :], in0=acc[R, :], scalar1=float(np.pi), scalar2=float(-TWO_PI), op0=ALU.is_gt, op1=ALU.mult),
                  reads=[t], writes=[t])
            mk.op("dve", lambda e: e.tensor_tensor(out=acc[R, :], in0=acc[R, :], in1=kf[R, :], op=ALU.add), reads=[t], writes=[t])
            mk.op("dve", lambda e: e.tensor_scalar(out=kf[R, :], in0=acc[R, :], scalar1=float(-np.pi), scalar2=float(TWO_PI), op0=ALU.is_lt, op1=ALU.mult),
                  reads=[t], writes=[t])
            mk.op("dve", lambda e: e.tensor_tensor(out=acc[R, :], in0=acc[R, :], in1=kf[R, :], op=ALU.add), reads=[t], writes=[t])
            mk.op("dve", lambda e: e.tensor_tensor(out=r2[R, :], in0=acc[R, :], in1=acc[R, :], op=ALU.mult), reads=[t], writes=[t])
            import math
            co = [(-1.0) ** k / math.factorial(2 * k + 1) for k in range(1, 9)]
            mk.op("dve", lambda e: e.tensor_scalar(out=kf[R, :], in0=r2[R, :], scalar1=float(co[-1]), scalar2=None, op0=ALU.mult),
                  reads=[t], writes=[t])
            for cc in reversed(co[:-1]):
                mk.op("dve", lambda e, cc=cc: e.scalar_tensor_tensor(out=kf[R, :], in0=kf[R, :], scalar=float(cc), in1=r2[R, :],
                                                                      op0=ALU.add, op1=ALU.mult), reads=[t], writes=[t])
            dst = sinT if which == 0 else cosT
            mk.op("dve", lambda e, dst=dst: e.scalar_tensor_tensor(out=dst[R, :], in0=kf[R, :], scalar=1.0, in1=acc[R, :],
                                                                  op0=ALU.add, op1=ALU.mult), reads=[t], writes=[t, ttab])
        mk.op("dve", lambda e: e.tensor_scalar(out=sinT[R, :], in0=sinT[R, :], scalar1=V("sgn")[R, :], scalar2=None, op0=ALU.mult),
              reads=[t, tvec, ttab], writes=[ttab])
        mk.close()

    def load_w(dst, src, tok):
        mk.dma("pool", dst, src, writes=[tok])

    def rstd_from(src_aps, src_toks, nfeat, ones_ap, sq, tsq, out_rstd, t_rstd, n=GS):
        ps, tp = bank()
        for i, (a, ta) in enumerate(zip(src_aps, src_toks)):
            mk.op("act", lambda e, a=a, i=i: e.activation(out=sq[:, i, 0:n], in_=a, func=AF.Square), reads=ta, writes=[tsq[i]])
        for i in range(len(src_aps)):
            mk.op("pe", lambda e, i=i: e.matmul(ps[:, 0:n], lhsT=ones_ap, rhs=sq[:, i, 0:n], start=(i == 0), stop=(i == len(src_aps) - 1)),
                  reads=[tsq[i], tconst], writes=[tp])
        mk.op("act", lambda e: e.activation(out=out_rstd[:, 0:n], in_=ps[:, 0:n], func=AF.Ln, scale=1.0 / nfeat, bias=EPS), reads=[tp], writes=[t_rstd])
        mk.op("act", lambda e: e.activation(out=out_rstd[:, 0:n], in_=out_rstd[:, 0:n], func=AF.Exp, scale=-0.5), reads=[t_rstd], writes=[t_rstd])

    def norm_group(g, gain_key, hT_g, th, sq, tsq, rstd, trs):
        rstd_from([xT[:, c, gs(g)] for c in range(8)], [[tx[c][g]] for c in range(8)], D, ones_b[:], sq, tsq, rstd, trs)
        for c in range(8):
            mk.op("dve", lambda e, c=c: e.scalar_tensor_tensor(out=hT_g[:, c, :], in0=xT[:, c, gs(g)], scalar=V(gain_key, 1, c), in1=rstd[:],
                                                               op0=ALU.mult, op1=ALU.mult), reads=[tx[c][g], trs, tvec], writes=[th[c]])

    def mlp(l):
        mk.open()
        hT = mk.sb("m_hT", [128, 8, S], BF16)
        th = [mk.toks(8) for _ in range(NG)]
        sq = mk.sb("m_sq", [128, 8, GS], BF16)
        tsq = mk.toks(8)
        rstd = mk.sb("m_rstd", [128, GS], F32)
        trs = mk.tok()
        w1s = [mk.sb("m_w1s%d" % i, [128, 8, 512], BF16) for i in range(2)]
        w2s = [mk.sb("m_w2s%d" % i, [128, 4, 1024], BF16) for i in range(2)]
        tw1 = mk.toks(2)
        tw2 = mk.toks(2)
        tmp = [mk.sb("m_tmp%d" % i, [128, GS], F32) for i in range(2)]
        ttmp = mk.toks(2)
        h1 = [mk.sb("m_h1%d" % i, [128, 4, GS], BF16) for i in range(2)]
        th1 = [mk.toks(4) for _ in range(2)]
        w1v = dr["m_w1"][l].rearrange("(kc p) n -> p kc n", p=128)
        w2v = dr["m_w2"][l].rearrange("(fc p) n -> p fc n", p=128)

        def loadslab(ffg):
            load_w(w1s[ffg % 2][:], w1v[:, :, ffg * 512:(ffg + 1) * 512], tw1[ffg % 2])
            load_w(w2s[ffg % 2][:], w2v[:, ffg * 4:(ffg + 1) * 4, :], tw2[ffg % 2])

        loadslab(0)
        for g in range(NG):
            class _H:
                pass
            norm_group(g, ("m_norm", l), hT[:, :, gs(g)], th[g], sq, tsq, rstd, trs)
        it = 0
        for ffg in range(8):
            if ffg + 1 < 8:
                loadslab(ffg + 1)
            a, b = w1s[ffg % 2], w2s[ffg % 2]
            for g in range(NG):
                hb = h1[it % 2]
                thb = th1[it % 2]
                it += 1
                for fc in range(4):
                    ps, tp = bank()
                    for kc in range(8):
                        mk.op("pe", lambda e, ps=ps, a=a, kc=kc, fc=fc, g=g: e.matmul(ps[:], lhsT=a[:, kc, fc * 128:(fc + 1) * 128], rhs=hT[:, kc, gs(g)],
                                                                                      start=(kc == 0), stop=(kc == 7)),
                              reads=[tw1[ffg % 2], th[g][kc]], writes=[tp])
                    tm, ttm = tmp[fc % 2], ttmp[fc % 2]
                    mk.op("act", lambda e, ps=ps, tm=tm: e.activation(out=tm[:], in_=ps[:], func=AF.Relu), reads=[tp], writes=[ttm])
                    mk.op("dve", lambda e, tm=tm, hb=hb, fc=fc: e.tensor_tensor(out=hb[:, fc, :], in0=tm[:], in1=tm[:], op=ALU.mult),
                          reads=[ttm], writes=[thb[fc]])
                for dc in range(8):
                    ps, tp = bank()
                    for fc in range(4):
                        mk.op("pe", lambda e, ps=ps, b=b, fc=fc, dc=dc, hb=hb: e.matmul(ps[:], lhsT=b[:, fc, dc * 128:(dc + 1) * 128], rhs=hb[:, fc, :],
                                                                                        start=(fc == 0), stop=(fc == 3)),
                              reads=[tw2[ffg % 2], thb[fc]], writes=[tp])
                    mk.op("dve", lambda e, ps=ps, dc=dc, g=g: e.tensor_tensor(out=xT[:, dc, gs(g)], in0=xT[:, dc, gs(g)], in1=ps[:], op=ALU.add),
                          reads=[tp, tx[dc][g]], writes=[tx[dc][g]])
        mk.close()

    def out_proj(wv, g, ymix, tym, wslab, twslab):
        for dc in range(8):
            ws, tws = wslab[dc % 2], twslab[dc % 2]
            load_w(ws[:], wv[:, :, dc * 128:(dc + 1) * 128], tws)
            ps, tp = bank()
            for kc in range(8):
                mk.op("pe", lambda e, ps=ps, ws=ws, kc=kc: e.matmul(ps[:], lhsT=ws[:, kc, :], rhs=ymix[:, kc, :], start=(kc == 0), stop=(kc == 7)),
                      reads=[tws, tym[kc]], writes=[tp])
            mk.op("dve", lambda e, ps=ps, dc=dc: e.tensor_tensor(out=xT[:, dc, gs(g)], in0=xT[:, dc, gs(g)], in1=ps[:], op=ALU.add),
                  reads=[tp, tx[dc][g]], writes=[tx[dc][g]])

    def attend(nchunks, k_ap_fn, q_ap, treads_k, treads_q, mask_fn, v_fn, tv, P, tP, pcount, ysb, tys, osb, tosb, rsb, trsb, hp):
        O, tO = obk()
        for c in range(nchunks):
            ps, tp = bank()
            m = mask_fn(c)
            mk.op("pe", lambda e, ps=ps, c=c, m=m: e.matmul(ps[:], lhsT=k_ap_fn(c), rhs=q_ap, start=True, stop=(m is None)),
                  reads=treads_k(c) + treads_q, writes=[tp])
            if m is not None:
                mk.op("pe", lambda e, ps=ps, m=m: e.matmul(ps[:], lhsT=ident_b[:], rhs=m[0], start=False, stop=True),
                      reads=m[1] + [tconst], writes=[tp])
            i = pcount[0] % len(P)
            pcount[0] += 1
            mk.op("act", lambda e, ps=ps, i=i: e.activation(out=P[i][:], in_=ps[:], func=AF.Exp), reads=[tp], writes=[tP[i]])
            mk.op("pe", lambda e, O=O, c=c, i=i: e.matmul(O[0:65, :], lhsT=v_fn(c), rhs=P[i][:], start=(c == 0), stop=(c == nchunks - 1)),
                  reads=[tP[i]] + tv(c), writes=[tO])
        mk.op("dve", lambda e, O=O: e.reciprocal(out=rsb[64:65, 0:GS], in_=O[64:65, :]), reads=[tO], writes=[trsb])
        mk.op("act", lambda e, O=O: e.activation(out=osb[0:64, 0:GS], in_=O[0:64, :], func=AF.Copy), reads=[tO], writes=[tosb])
        ps, tp = bank()
        mk.op("pe", lambda e, ps=ps: e.matmul(ps[0:64, :], lhsT=ones_f[64:65, 0:64], rhs=rsb[64:65, 0:GS], start=True, stop=True),
              reads=[trsb, tconst], writes=[tp])
        mk.op("dve", lambda e, ps=ps: e.tensor_tensor(out=ysb[0:64, :], in0=osb[0:64, 0:GS], in1=ps[0:64, :], op=ALU.mult),
              reads=[tosb, tp], writes=[tys])

    def even_mixer(j):
        mk.open()
        SC_Q = 96.0 ** -0.5
        hT = mk.sb("e_hT", [128, 8, GS], BF16)
        th = mk.toks(8)
        sq = mk.sb("e_sq", [128, 8, GS], BF16)
        tsq = mk.toks(8)
        rstd = mk.sb("e_rstd", [128, GS], F32)
        trs = mk.tok()
        wsl = [mk.sb("e_wsl%d" % i, [128, 8, 128], BF16) for i in range(2)]
        twsl = mk.toks(2)
        wkr = mk.sb("e_wkr", [128, 8, 192], BF16)
        twkr = mk.tok()
        wuq = mk.sb("e_wuq", [128, 2, 768], BF16)
        wuqs = mk.sb("e_wuqs", [128, 2, 768], BF16)
        wuk = mk.sb("e_wuk", [128, 512], BF16)
        wuv = mk.sb("e_wuv", [128, 512], BF16)
        gabd = mk.sb("e_gabd", [128, 4, 128], BF16)
        gxbd = mk.sb("e_gxbd", [128, 4, 128], BF16)
        tw = mk.tok()
        wosl = [mk.sb("e_wosl%d" % i, [128, 8, 128], BF16) for i in range(2)]
        twosl = mk.toks(2)
        clam = mk.sb("e_clam", [128, 4], F32)
        tclam = mk.tok()
        ymix = mk.sb("e_ymix", [128, 8, GS], BF16)
        tym = mk.toks(8)
        XR = mk.sb("e_XR", [128, GS + 3], F32)
        halo = mk.sb("e_halo", [128, 4, 3], F32)
        thalo = mk.toks(4)
        hlast = mk.sb("e_hlast", [128, 4], F32)
        thl = mk.toks(4)
        XC = mk.sb("e_XC", [128, GS], F32)
        XCb = mk.sb("e_XCb", [128, GS], BF16)
        GG = mk.sb("e_GG", [128, GS], F32)
        TA = mk.sb("e_TA", [128, GS], F32)
        TI = mk.sb("e_TI", [128, GS], F32)
        TM = mk.sb("e_TM", [128, GS], F32)
        HH = mk.sb("e_HH", [128, GS], F32)
        tXR, tXC, tXCb, tGG, tTA, tTI, tTM, tHH = mk.toks(8)
        cqn = mk.sb("e_cqn", [128, 2, GS], BF16)
        tcqn = mk.toks(2)
        ckvn = mk.sb("e_ckvn", [128, S], BF16)
        tckvn = mk.toks(NG)
        KR = mk.sb("e_KR", [96, S], BF16)
        tKR = mk.toks(NG)
        Vg = mk.sb("e_V", [128, 16, 8, 65], BF16)
        tV = mk.toks(16)
        kT = [mk.sb("e_kT0", [96, S], BF16)] * 2
        tkT = [mk.toks(NG)] * 2
        qT = [mk.sb("e_qT%d" % i, [96, GS], BF16) for i in range(2)]
        tqT = mk.toks(2)
        P = [mk.sb("e_P%d" % i, [128, GS], BF16) for i in range(3)]
        tP = mk.toks(3)
        pcount = [0]
        ysb = [mk.sb("e_ysb%d" % i, [64, GS], BF16) for i in range(2)]
        tys = mk.toks(2)
        rsb = mk.sb("e_rsb", [65, GS], F32)
        trsb = mk.tok()
        R = slice(64, 96)
        CQl = [TA, TI]
        tCQ = [tTA, tTI]
        CKV, tCKV = TA, tTA
        KA, tKA, KB, tKB = TM, tTM, HH, tHH
        QA, tQA, QB, tQB = XC, tXC, GG, tGG
        osb, tosb = XR, tXR

        load_w(wuq[:], dr["e_w_uq"][j].rearrange("(kc p) n -> p kc n", p=128), tw)
        load_w(wuqs[:], dr["e_w_uq_sw"][j].rearrange("(kc p) n -> p kc n", p=128), tw)
        load_w(wuk[:], dr["e_w_uk"][j], tw)
        load_w(wuv[:], dr["e_w_uv"][j], tw)
        load_w(gabd[:], dr["e_ga_bd"][j].rearrange("c p n -> p c n"), tw)
        load_w(gxbd[:], dr["e_gx_bd"][j].rearrange("c p n -> p c n"), tw)
        load_w(wkr[:], dr["e_w_kr"][j].rearrange("(kc p) n -> p kc n", p=128), twkr)
        mk.op("pool", lambda e: e.memset(Vg[:, :, :, 64:65], 1.0), writes=tV)
        mk.op("act", lambda e: e.activation(out=clam[:], in_=V(("lam", j), 4), func=AF.Exp, scale=-1.0), reads=[tvec], writes=[tclam])
        mk.op("act", lambda e: e.activation(out=clam[:], in_=clam[:], func=AF.Ln, bias=1.0), reads=[tclam], writes=[tclam])
        mk.op("dve", lambda e: e.tensor_scalar(out=clam[:], in0=clam[:], scalar1=-8.0, scalar2=None, op0=ALU.mult), reads=[tclam], writes=[tclam])

        w_in = dr["e_w_in"][j].rearrange("(kc p) n -> p kc n", p=128)
        wov = dr["e_w_out"][j].rearrange("(kc p) n -> p kc n", p=128)
        slab_i = [0]

        def proj(col0, ncol, g, lhs_slab=None, tslab=None):
            if lhs_slab is None:
                i = slab_i[0] % 2
                slab_i[0] += 1
                lhs_slab, tslab = wsl[i], twsl[i]
                load_w(lhs_slab[:, :, 0:ncol], w_in[:, :, col0:col0 + ncol], tslab)
                c0 = 0
            else:
                c0 = col0
            ps, tp = bank()
            for kc in range(8):
                mk.op("pe", lambda e, ps=ps, kc=kc, c0=c0, s=lhs_slab: e.matmul(ps[0:ncol, :], lhsT=s[:, kc, c0:c0 + ncol], rhs=hT[:, kc, :],
                                                                                start=(kc == 0), stop=(kc == 7)),
                      reads=[tslab, th[kc]], writes=[tp])
            return ps, tp

        for g in range(NG):
            norm_group(g, ("e_norm", j), hT, th, sq, tsq, rstd, trs)
            for c in range(4):
                ps, tp = proj(c * 128, 128, g)
                if g == 0:
                    mk.op("dve", lambda e: e.memset(XR[:, 0:3], 0.0), writes=[tXR])
                else:
                    mk.op("dve", lambda e, c=c: e.tensor_copy(out=XR[:, 0:3], in_=halo[:, c, :]), reads=[thalo[c]], writes=[tXR])
                mk.op("act", lambda e, ps=ps: e.activation(out=XR[:, 3:GS + 3], in_=ps[:], func=AF.Copy), reads=[tp], writes=[tXR])
                mk.op("dve", lambda e, c=c: e.tensor_copy(out=halo[:, c, :], in_=XR[:, GS:GS + 3]), reads=[tXR], writes=[thalo[c]])
                ps2, tp2 = proj(512 + c * 128, 128, g)
                mk.op("act", lambda e, ps2=ps2: e.activation(out=GG[:], in_=ps2[:], func=AF.Gelu_apprx_tanh), reads=[tp2], writes=[tGG])
                cw = VEC[("conv_w", j)]
                mk.op("dve", lambda e, c=c: e.tensor_scalar(out=XC[:], in0=XR[:, 3:GS + 3], scalar1=V(cw + 12 + c), scalar2=V(("conv_b", j), 1, c),
                                                            op0=ALU.mult, op1=ALU.add), reads=[tXR, tvec], writes=[tXC])
                for t in range(3):
                    mk.op("dve", lambda e, c=c, t=t: e.scalar_tensor_tensor(out=XC[:], in0=XR[:, t:GS + t], scalar=V(cw + 4 * t + c), in1=XC[:],
                                                                            op0=ALU.mult, op1=ALU.add), reads=[tXR, tvec], writes=[tXC])
                mk.op("act", lambda e: e.activation(out=XCb[:], in_=XC[:], func=AF.Copy), reads=[tXC], writes=[tXCb])
                psa, tpa = bank()
                mk.op("pe", lambda e, psa=psa, c=c: e.matmul(psa[:], lhsT=gabd[:, c, :], rhs=XCb[:], start=True, stop=True), reads=[tw, tXCb], writes=[tpa])
                psx, tpx = bank()
                mk.op("pe", lambda e, psx=psx, c=c: e.matmul(psx[:], lhsT=gxbd[:, c, :], rhs=XCb[:], start=True, stop=True), reads=[tw, tXCb], writes=[tpx])
                mk.op("act", lambda e, psa=psa, c=c: e.activation(out=TA[:], in_=psa[:], func=AF.Sigmoid, bias=V(("ga_b", j), 1, c)), reads=[tpa, tvec], writes=[tTA])
                mk.op("act", lambda e, psx=psx, c=c: e.activation(out=TI[:], in_=psx[:], func=AF.Sigmoid, bias=V(("gx_b", j), 1, c)), reads=[tpx, tvec], writes=[tTI])
                mk.op("act", lambda e, c=c: e.activation(out=TA[:], in_=TA[:], func=AF.Exp, scale=clam[:, c:c + 1]), reads=[tTA, tclam], writes=[tTA])
                mk.op("act", lambda e: e.activation(out=TM[:], in_=TA[:], func=AF.Square), reads=[tTA], writes=[tTM])
                mk.op("act", lambda e: e.activation(out=TM[:], in_=TM[:], func=AF.Ln, scale=-1.0, bias=1.0), reads=[tTM], writes=[tTM])
                mk.op("act", lambda e: e.activation(out=TM[:], in_=TM[:], func=AF.Exp, scale=0.5), reads=[tTM], writes=[tTM])
                mk.op("dve", lambda e: e.tensor_tensor(out=TI[:], in0=TI[:], in1=XC[:], op=ALU.mult), reads=[tTI, tXC], writes=[tTI])
                mk.op("dve", lambda e: e.tensor_tensor(out=TI[:], in0=TI[:], in1=TM[:], op=ALU.mult), reads=[tTI, tTM], writes=[tTI])
                init = 0.0 if g == 0 else hlast[:, c:c + 1]
                mk.op("dve", lambda e, init=init: e.tensor_tensor_scan(out=HH[:], data0=TA[:], data1=TI[:], initial=init, op0=ALU.mult, op1=ALU.add),
                      reads=[tTA, tTI, thl[c]], writes=[tHH])
                mk.op("dve", lambda e, c=c: e.tensor_copy(out=hlast[:, c:c + 1], in_=HH[:, GS - 1:GS]), reads=[tHH], writes=[thl[c]])
                mk.op("dve", lambda e, c=c: e.tensor_tensor(out=ymix[:, c, :], in0=HH[:], in1=GG[:], op=ALU.mult), reads=[tHH, tGG], writes=[tym[c]])
            for q in range(2):
                ps, tp = proj(1024 + q * 128, 128, g)
                mk.op("act", lambda e, ps=ps, q=q: e.activation(out=CQl[q][:], in_=ps[:], func=AF.Copy), reads=[tp], writes=[tCQ[q]])
            rstd_from([CQl[0][:], CQl[1][:]], [[tCQ[0]], [tCQ[1]]], 256, ones_b[:], sq, tsq, rstd, trs)
            for q in range(2):
                mk.op("dve", lambda e, q=q: e.scalar_tensor_tensor(out=cqn[:, q, :], in0=CQl[q][:], scalar=V(("q_norm", j), 1, q), in1=rstd[:],
                                                                   op0=ALU.mult, op1=ALU.mult), reads=[tCQ[q], trs, tvec], writes=[tcqn[q]])
            ps, tp = proj(1280, 128, g)
            mk.op("act", lambda e, ps=ps: e.activation(out=CKV[:], in_=ps[:], func=AF.Copy), reads=[tp], writes=[tCKV])
            rstd_from([CKV[:]], [[tCKV]], 128, ones_b[:], sq, tsq, rstd, trs)
            mk.op("dve", lambda e, g=g: e.scalar_tensor_tensor(out=ckvn[:, gs(g)], in0=CKV[:], scalar=V(("kv_norm", j)), in1=rstd[:],
                                                               op0=ALU.mult, op1=ALU.mult), reads=[tCKV, trs, tvec], writes=[tckvn[g]])
            psA, tpA = proj(0, 96, g, wkr, twkr)
            psB, tpB = proj(96, 96, g, wkr, twkr)
            mk.op("dve", lambda e, psA=psA, g=g: e.tensor_tensor(out=KA[R, :], in0=psA[R, :], in1=cosT[R, gs(g)], op=ALU.mult), reads=[tpA, ttab], writes=[tKA])
            mk.op("dve", lambda e, psB=psB, g=g: e.tensor_tensor(out=KB[R, :], in0=psB[R, :], in1=sinT[R, gs(g)], op=ALU.mult), reads=[tpB, ttab], writes=[tKB])
            mk.op("dve", lambda e, g=g: e.tensor_tensor(out=KR[R, gs(g)], in0=KA[R, :], in1=KB[R, :], op=ALU.add), reads=[tKA, tKB], writes=[tKR[g]])
            for tt in range(4 * g, 4 * g + 4):
                ps, tp = bank()
                mk.op("pe", lambda e, ps=ps, tt=tt: e.matmul(ps[:], lhsT=ckvn[:, tt * 128:(tt + 1) * 128], rhs=wuv[:], start=True, stop=True),
                      reads=[tckvn[g], tw], writes=[tp])
                mk.op("act", lambda e, ps=ps, tt=tt: e.activation(out=Vg[:, tt, :, 0:64], in_=ps[:].rearrange("p (h d) -> p h d", h=8), func=AF.Copy),
                      reads=[tp], writes=[tV[tt]])
            nch = 4 * (g + 1)
            for h in range(8):
                kb, tkb = kT[h % 2], tkT[h % 2]
                for g2 in range(g + 1):
                    ps, tp = bank()
                    mk.op("pe", lambda e, ps=ps, g2=g2, h=h: e.matmul(ps[0:64, :], lhsT=wuk[:, h * 64:(h + 1) * 64], rhs=ckvn[:, gs(g2)], start=True, stop=True),
                          reads=[tw, tckvn[g2]], writes=[tp])
                    mk.op("act", lambda e, ps=ps, g2=g2, kb=kb: e.activation(out=kb[0:64, gs(g2)], in_=ps[0:64, :], func=AF.Copy), reads=[tp], writes=[tkb[g2]])
                    mk.op("dve", lambda e, g2=g2, kb=kb: e.tensor_copy(out=kb[R, gs(g2)], in_=KR[R, gs(g2)]), reads=[tKR[g2]], writes=[tkb[g2]])
                qb, tqb = qT[h % 2], tqT[h % 2]
                psA, tpA = bank()
                psB, tpB = bank()
                for q in range(2):
                    mk.op("pe", lambda e, psA=psA, q=q, h=h: e.matmul(psA[0:96, :], lhsT=wuq[:, q, h * 96:(h + 1) * 96], rhs=cqn[:, q, :], start=(q == 0), stop=(q == 1)),
                          reads=[tw, tcqn[q]], writes=[tpA])
                for q in range(2):
                    mk.op("pe", lambda e, psB=psB, q=q, h=h: e.matmul(psB[0:96, :], lhsT=wuqs[:, q, h * 96:(h + 1) * 96], rhs=cqn[:, q, :], start=(q == 0), stop=(q == 1)),
                          reads=[tw, tcqn[q]], writes=[tpB])
                mk.op("act", lambda e, psA=psA, qb=qb: e.activation(out=qb[0:64, :], in_=psA[0:64, :], func=AF.Copy, scale=SC_Q), reads=[tpA], writes=[tqb])
                mk.op("dve", lambda e, psA=psA, g=g: e.tensor_tensor(out=QA[R, :], in0=psA[R, :], in1=cosT[R, gs(g)], op=ALU.mult), reads=[tpA, ttab], writes=[tQA])
                mk.op("dve", lambda e, psB=psB, g=g: e.tensor_tensor(out=QB[R, :], in0=psB[R, :], in1=sinT[R, gs(g)], op=ALU.mult), reads=[tpB, ttab], writes=[tQB])
                mk.op("dve", lambda e: e.tensor_scalar(out=QB[R, :], in0=QB[R, :], scalar1=SC_Q, scalar2=None, op0=ALU.mult), reads=[tQB], writes=[tQB])
                mk.op("dve", lambda e, qb=qb: e.scalar_tensor_tensor(out=qb[R, :], in0=QA[R, :], scalar=SC_Q, in1=QB[R, :], op0=ALU.mult, op1=ALU.add),
                      reads=[tQA, tQB], writes=[tqb])
                ys, tysi = ysb[h % 2], tys[h % 2]
                attend(nch,
                       lambda c, kb=kb: kb[:, c * 128:(c + 1) * 128],
                       qb[:], lambda c, tkb=tkb: [tkb[c // 4]], [tqb],
                       lambda c, g=g: ((maskM[:, c - 4 * g, :], [tconst]) if c >= 4 * g else None),
                       lambda c, h=h: Vg[:, c, h, :], lambda c: [tV[c]],
                       P, tP, pcount, ys, tysi, osb, tosb, rsb, trsb, 0)
                hp = (h % 2) * 64
                mk.dma("sp", ymix[hp:hp + 64, 4 + h // 2, :], ys[0:64, :], reads=[tysi], writes=[tym[4 + h // 2]])
            out_proj(wov, g, ymix, tym, wosl, twosl)
        mk.close()

    def odd_mixer(j):
        mk.open()
        SC_Q = 64.0 ** -0.5
        hT = mk.sb("o_hT", [128, 8, GS], BF16)
        th = mk.toks(8)
        sq = mk.sb("o_sq", [128, 8, GS], BF16)
        tsq = mk.toks(8)
        rstd = mk.sb("o_rstd", [128, GS], F32)
        trs = mk.tok()
        wsl = [mk.sb("o_wsl%d" % i, [128, 8, 256], BF16) for i in range(2)]
        twsl = mk.toks(2)
        wosl = [mk.sb("o_wosl%d" % i, [128, 8, 128], BF16) for i in range(2)]
        twosl = mk.toks(2)
        ymix = mk.sb("o_ymix", [128, 8, GS], BF16)
        tym = mk.toks(8)
        Kt = mk.sb("o_K", [128, 2, S], BF16)
        tK = [mk.toks(NG) for _ in range(2)]
        Vg = mk.sb("o_V", [128, 16, 4, 65], BF16)
        tV = mk.toks(16)
        KI = mk.sb("o_KI", [128, GS], F32)
        tKI = mk.tok()
        kin = mk.sb("o_kin", [128, S], BF16)
        tkin = mk.toks(NG)
        qg = mk.sb("o_qg", [128, 8, GS], BF16)
        tqg = mk.toks(8)
        qig = mk.sb("o_qig", [128, 4, GS], BF16)
        tqig = mk.toks(4)
        WI = mk.sb("o_WI", [128, 4, 8], F32)
        WA = mk.sb("o_WA", [128, 4, 8], F32)
        WS = mk.sb("o_WS", [128, 4, 8], F32)
        tWI = mk.toks(4)
        maskT = mk.sb("o_maskT", [128, 16, GS], BF16)
        tmT = [mk.toks(4) for _ in range(16)]
        Sidx = [mk.sb("o_Sidx0", [128, S], F32)] * 2
        tSx = [mk.tok()] * 2
        rl = [mk.sb("o_rl%d" % i, [128, GS], F32) for i in range(2)]
        trl = mk.toks(2)
        mb = mk.sb("o_mb", [128, S], BF16)
        tmb = mk.tok()
        bs = mk.sb("o_bs", [128, 8], F32)
        tbs = mk.tok()
        Wb = mk.sb("o_Wb", [128, NBIS], F32)
        ptr = mk.ps("o_ptr", [128, 1024], BF16)
        tptr = mk.tok()
        P = [mk.sb("o_P%d" % i, [128, GS], BF16) for i in range(3)]
        tP = mk.toks(3)
        pcount = [0]
        ysb = [mk.sb("o_ysb%d" % i, [64, GS], BF16) for i in range(2)]
        tys = mk.toks(2)
        rsb = mk.sb("o_rsb", [65, GS], F32)
        trsb = mk.tok()

        junk, tjunk = mb, tmb
        osb, tosb = KI, tKI
        w_in = dr["o_w_in"][j].rearrange("(kc p) n -> p kc n", p=128)
        wov = dr["o_w_out"][j].rearrange("(kc p) n -> p kc n", p=128)
        slab_i = [0]
        mk.op("pool", lambda e: e.memset(Vg[:, :, :, 64:65], 1.0), writes=tV)

        def slab(col0, ncol):
            i = slab_i[0] % 2
            slab_i[0] += 1
            load_w(wsl[i][:, :, 0:ncol], w_in[:, :, col0:col0 + ncol], twsl[i])
            return wsl[i], twsl[i]

        def projf(s, ts, c0, g_unused=None):
            ps, tp = bank()
            for kc in range(8):
                mk.op("pe", lambda e, ps=ps, kc=kc: e.matmul(ps[:], lhsT=s[:, kc, c0:c0 + 128], rhs=hT[:, kc, :], start=(kc == 0), stop=(kc == 7)),
                      reads=[ts, th[kc]], writes=[tp])
            return ps, tp

        for g in range(NG):
            norm_group(g, ("o_norm", j), hT, th, sq, tsq, rstd, trs)
            s, ts = slab(1024, 256)
            for m in range(2):
                ps, tp = projf(s, ts, m * 128)
                mk.op("act", lambda e, ps=ps, m=m, g=g: e.activation(out=Kt[:, m, gs(g)], in_=ps[:], func=AF.Copy), reads=[tp], writes=[tK[m][g]])
            s, ts = slab(1280, 256)
            for tt in range(4):
                ps, tp = bank()
                for kc in range(8):
                    mk.op("pe", lambda e, ps=ps, kc=kc, tt=tt, s=s: e.matmul(ps[:, 0:256], lhsT=hT[:, kc, tt * 128:(tt + 1) * 128], rhs=s[:, kc, 0:256],
                                                                             start=(kc == 0), stop=(kc == 7)), reads=[ts, th[kc]], writes=[tp])
                mk.op("act", lambda e, ps=ps, tt=tt, g=g: e.activation(out=Vg[:, 4 * g + tt, :, 0:64], in_=ps[:, 0:256].rearrange("p (h d) -> p h d", h=4), func=AF.Copy),
                      reads=[tp], writes=[tV[4 * g + tt]])
            s, ts = slab(2048, 136)
            ps, tp = projf(s, ts, 0)
            mk.op("act", lambda e, ps=ps: e.activation(out=KI[:], in_=ps[:], func=AF.Copy), reads=[tp], writes=[tKI])
            rstd_from([KI[:]], [[tKI]], 64, bd1_b[:], sq, tsq, rstd, trs)
            mk.op("dve", lambda e, g=g: e.scalar_tensor_tensor(out=kin[:, gs(g)], in0=KI[:], scalar=V(("ikn", j)), in1=rstd[:], op0=ALU.mult, op1=ALU.mult),
                  reads=[tKI, trs, tvec], writes=[tkin[g]])
            for tt in range(4):
                ps, tp = bank()
                for kc in range(8):
                    mk.op("pe", lambda e, ps=ps, kc=kc, tt=tt, s=s: e.matmul(ps[:, 0:8], lhsT=hT[:, kc, tt * 128:(tt + 1) * 128], rhs=s[:, kc, 128:136],
                                                                             start=(kc == 0), stop=(kc == 7)), reads=[ts, th[kc]], writes=[tp])
                mk.op("act", lambda e, ps=ps, tt=tt: e.activation(out=WI[:, tt, :], in_=ps[:, 0:8], func=AF.Copy), reads=[tp], writes=[tWI[tt]])
                mk.op("act", lambda e, tt=tt: e.activation(out=WA[:, tt, :], in_=WI[:, tt, :], func=AF.Abs), reads=[tWI[tt]], writes=[tWI[tt]])
                mk.op("act", lambda e, tt=tt: e.activation(out=WS[:, tt, :], in_=WI[:, tt, :], func=AF.Sign), reads=[tWI[tt]], writes=[tWI[tt]])
            for half in range(4):
                s, ts = slab(half * 256, 256)
                for m in range(2):
                    ps, tp = projf(s, ts, m * 128)
                    jj = 2 * half + m
                    mk.op("act", lambda e, ps=ps, jj=jj: e.activation(out=qg[:, jj, :], in_=ps[:], func=AF.Copy, scale=SC_Q), reads=[tp], writes=[tqg[jj]])
            for half in range(2):
                s, ts = slab(1536 + half * 256, 256)
                for m in range(2):
                    ps, tp = projf(s, ts, m * 128)
                    jj = 2 * half + m
                    mk.op("act", lambda e, ps=ps, jj=jj: e.activation(out=qig[:, jj, :], in_=ps[:], func=AF.Copy), reads=[tp], writes=[tqig[jj]])
            nch = 4 * (g + 1)
            for tt in range(4):
                i = 4 * g + tt
                qs = slice(tt * 128, (tt + 1) * 128)
                if i < 2:
                    for c in range(nch):
                        mk.op("pool", lambda e, c=c, qs=qs: e.tensor_copy(out=maskT[:, c, qs], in_=maskM[:, c, qs]), reads=[tconst], writes=[tmT[c][tt]])
                    continue
                kl = 128 * (i + 1)
                Sx, tS = Sidx[i % 2], tSx[i % 2]
                for kb in range((kl + 511) // 512):
                    n = min(512, kl - 512 * kb)
                    ks = slice(512 * kb, 512 * kb + n)
                    for hh in range(8):
                        hp = slice((hh % 2) * 64, (hh % 2) * 64 + 64)
                        ps, tp = bank()
                        mk.op("pe", lambda e, ps=ps, hh=hh, hp=hp, qs=qs, ks=ks, n=n: e.matmul(ps[:, 0:n], lhsT=qig[hp, hh // 2, qs], rhs=kin[hp, ks], start=True, stop=True),
                              reads=[tqig[hh // 2]] + [tkin[x] for x in range(kb, kb + 1)], writes=[tp])
                        r_, tr_ = rl[hh % 2], trl[hh % 2]
                        mk.op("act", lambda e, ps=ps, r_=r_, tt=tt, hh=hh, n=n: e.activation(out=r_[:, 0:n], in_=ps[:, 0:n], func=AF.Relu, scale=WA[:, tt, hh:hh + 1]),
                              reads=[tp, tWI[tt]], writes=[tr_])
                        if hh == 0:
                            mk.op("dve", lambda e, r_=r_, Sx=Sx, ks=ks, tt=tt, n=n: e.tensor_scalar(out=Sx[:, ks], in0=r_[:, 0:n], scalar1=WS[:, tt, 0:1], scalar2=None, op0=ALU.mult),
                                  reads=[tr_, tWI[tt]], writes=[tS])
                        else:
                            mk.op("dve", lambda e, r_=r_, Sx=Sx, ks=ks, tt=tt, hh=hh, n=n: e.scalar_tensor_tensor(out=Sx[:, ks], in0=r_[:, 0:n], scalar=WS[:, tt, hh:hh + 1], in1=Sx[:, ks],
                                                                                                               op0=ALU.mult, op1=ALU.add), reads=[tr_, tWI[tt]], writes=[tS])
                SV = Sx[:, 0:kl]
                mk.op("dve", lambda e, SV=SV: e.tensor_reduce(out=bs[:, 0:1], in_=SV, axis=AX.X, op=ALU.max), reads=[tS], writes=[tbs])
                mk.op("dve", lambda e, SV=SV: e.tensor_reduce(out=bs[:, 1:2], in_=SV, axis=AX.X, op=ALU.min), reads=[tS], writes=[tbs])
                mk.op("dve", lambda e, Sx=Sx, kl=kl: e.tensor_tensor(out=Sx[:, kl - 128:kl], in0=Sx[:, kl - 128:kl], in1=cm_f[:], op=ALU.add), reads=[tS, tconst], writes=[tS])
                mk.op("dve", lambda e: e.tensor_tensor(out=bs[:, 2:3], in0=bs[:, 0:1], in1=bs[:, 1:2], op=ALU.subtract), reads=[tbs], writes=[tbs])
                mk.op("dve", lambda e: e.tensor_scalar(out=Wb[:], in0=pow_f[:], scalar1=bs[:, 2:3], scalar2=None, op0=ALU.mult), reads=[tbs, tconst], writes=[tbs])
                mk.op("dve", lambda e: e.scalar_tensor_tensor(out=bs[:, 3:4], in0=bs[:, 2:3], scalar=0.5, in1=bs[:, 1:2], op0=ALU.mult, op1=ALU.add), reads=[tbs], writes=[tbs])
                for it in range(NBIS):
                    mk.op("dve", lambda e, SV=SV, kl=kl: e.tensor_scalar(out=junk[:, 0:kl], in0=SV, scalar1=bs[:, 3:4], scalar2=0.0, op0=ALU.is_gt, op1=ALU.add, accum_out=bs[:, 4:5]),
                          reads=[tS, tbs], writes=[tjunk, tbs])
                    mk.op("dve", lambda e: e.tensor_scalar(out=bs[:, 5:6], in0=bs[:, 4:5], scalar1=float(TOPK) - 0.5, scalar2=0.5, op0=ALU.is_gt, op1=ALU.subtract),
                          reads=[tbs], writes=[tbs])
                    mk.op("dve", lambda e, it=it: e.scalar_tensor_tensor(out=bs[:, 3:4], in0=bs[:, 5:6], scalar=Wb[:, it:it + 1], in1=bs[:, 3:4], op0=ALU.mult, op1=ALU.add),
                          reads=[tbs], writes=[tbs])
                mk.op("dve", lambda e, SV=SV, kl=kl: e.tensor_scalar(out=mb[:, 0:kl], in0=SV, scalar1=bs[:, 3:4], scalar2=NEG, op0=ALU.is_le, op1=ALU.mult),
                      reads=[tS, tbs], writes=[tmb])
                for c0 in range(0, i + 1, 8):
                    c1 = min(i + 1, c0 + 8)
                    for c in range(c0, c1):
                        mk.op("pe", lambda e, c=c, c0=c0: e.transpose(out=ptr[:, (c - c0) * 128:(c - c0 + 1) * 128], in_=mb[:, c * 128:(c + 1) * 128], identity=ident_b[:]),
                              reads=[tmb, tconst], writes=[tptr])
                    mk.op("act", lambda e, c0=c0, c1=c1, qs=qs: e.activation(out=maskT[:, c0:c1, qs], in_=ptr[:, 0:(c1 - c0) * 128].rearrange("p (c q) -> p c q", q=128), func=AF.Copy),
                          reads=[tptr], writes=[tmT[c][tt] for c in range(c0, c1)])
                for c in range(i + 1, nch):
                    mk.op("pool", lambda e, c=c, qs=qs: e.memset(maskT[:, c, qs], NEG), writes=[tmT[c][tt]])
            for jj in range(8):
                for half in range(2):
                    hp = slice(half * 64, half * 64 + 64)
                    kvh = 2 * (jj // 4) + half
                    hidx = 2 * jj + half
                    ys, tysi = ysb[hidx % 2], tys[hidx % 2]
                    attend(nch,
                           lambda c, hp=hp, jj=jj: Kt[hp, jj // 4, c * 128:(c + 1) * 128],
                           qg[hp, jj, :], lambda c, jj=jj: [tK[jj // 4][c // 4]], [tqg[jj]],
                           lambda c: (maskT[:, c, :], tmT[c]),
                           lambda c, kvh=kvh: Vg[:, c, kvh, :], lambda c: [tV[c]],
                           P, tP, pcount, ys, tysi, osb, tosb, rsb, trsb, 0)
                    mk.dma("sp", ymix[hp, jj, :], ys[0:64, :], reads=[tysi], writes=[tym[jj]])
            out_proj(wov, g, ymix, tym, wosl, twosl)
        mk.close()

    def final_out(do_norm):
        mk.open()
        sq = mk.sb("f_sq", [128, 8, GS], BF16)
        tsq = mk.toks(8)
        rstd = mk.sb("f_rstd", [128, GS], F32)
        trs = mk.tok()
        ob = [mk.sb("f_ob%d" % i, [128, GS], F32) for i in range(2)]
        tob = mk.toks(2)
        touts = []
        n = 0
        for g in range(NG):
            if do_norm:
                rstd_from([xT[:, c, gs(g)] for c in range(8)], [[tx[c][g]] for c in range(8)], D, ones_b[:], sq, tsq, rstd, trs)
            for c in range(8):
                if do_norm:
                    o, to = ob[n % 2], tob[n % 2]
                    n += 1
                    mk.op("dve", lambda e, c=c, o=o, g=g: e.scalar_tensor_tensor(out=o[:], in0=xT[:, c, gs(g)], scalar=V("final", 1, c), in1=rstd[:],
                                                                                 op0=ALU.mult, op1=ALU.mult), reads=[tx[c][g], trs, tvec], writes=[to])
                    tdone = mk.tok()
                    mk.dma("sp", yout[c * 128:(c + 1) * 128, gs(g)], o[:], reads=[to], writes=[tdone])
                else:
                    tdone = mk.tok()
                    mk.dma("sp", yout[c * 128:(c + 1) * 128, gs(g)], xT[:, c, gs(g)], reads=[tx[c][g]], writes=[tdone])
                touts.append(tdone)
        mk.finish(touts)
        mk.close()

    need_rope = any(s[0] == "x" and int(s[1]) % 2 == 0 for s in stages)
    if need_rope:
        rope_tables()
    for s in stages:
        if s[0] == "x":
            l = int(s[1])
            if l % 2 == 0:
                even_mixer(l // 2)
            else:
                odd_mixer(l // 2)
        elif s[0] == "m":
            mlp(int(s[1]))
    final_out("f" in stages)
    mk.emit()
    return nc


FULL = ["x0", "m0", "x1", "m1", "x2", "m2", "x3", "m3", "f"]
_CACHE = {}


def kernel(**inputs):
    stages = os.environ.get("MK_STAGES")
    stages = stages.split(",") if stages else FULL
    x = np.asarray(inputs["x"], np.float32)
    pos = np.asarray(inputs["positions"], np.int32)
    sh = prep_shared(inputs)
    key = tuple(stages)
    if not any(st[0] == "m" for st in stages):
        sh["m_w1"] = np.zeros((1, 128, 128), np.float32)
        sh["m_w2"] = np.zeros((1, 128, 128), np.float32)
    if key not in _CACHE:
        _CACHE[key] = build(stages)
    nc = _CACHE[key]
    in_maps = []
    for b in range(8):
        m = dict(sh)
        m["xT"] = np.ascontiguousarray(x[b].T)
        m["pos"] = np.ascontiguousarray(pos[b][None, :])
        in_maps.append(m)
    res = run_bass_kernel_spmd(nc, in_maps, core_ids=list(range(8)))
    out = np.stack([np.asarray(res.results[b]["yT"], np.float32).T for b in range(8)], axis=0)
    return np.ascontiguousarray(out)
```

```python
import os
from contextlib import ExitStack
import numpy as np
import concourse.bass as bass
import concourse.mybir as mybir
from concourse.bass_utils import run_bass_kernel_spmd

F32 = mybir.dt.float32
BF16 = mybir.dt.bfloat16
I32 = mybir.dt.int32
AF = mybir.ActivationFunctionType
ALU = mybir.AluOpType
AX = mybir.AxisListType

S = 2048
D = 1024
NG = 4
GS = 512
EPS = 1e-6
NEG = -30000.0
TOPK = 256
NBIS = 16


class Tok:
    __slots__ = ("w", "r")

    def __init__(self, fence):
        self.w = None
        self.r = dict(fence)


class MK:
    ENG = ["pe", "act", "dve", "pool", "sp"]
    NDMA = 24

    def __init__(self, nc):
        self.nc = nc
        self.prog = {e: [] for e in self.ENG}
        self.cnt = {e: 0 for e in self.ENG}
        self.known = {e: {} for e in self.ENG}
        self.dma_n = 0
        self.dma_last = [0] * self.NDMA
        self.stacks = [ExitStack()]
        self.scope_toks = [[]]
        self.fence = {}
        self.nbank = 0

    def sb(self, name, shape, dt):
        self.nname = getattr(self, "nname", 0) + 1
        return self.stacks[-1].enter_context(self.nc.sbuf_tensor("%s_%d" % (name, self.nname), list(shape), dt))

    def ps(self, name, shape, dt=F32):
        self.nname = getattr(self, "nname", 0) + 1
        return self.stacks[-1].enter_context(self.nc.psum_tensor("%s_%d" % (name, self.nname), list(shape), dt))

    def tok(self):
        t = Tok(self.fence)
        self.scope_toks[-1].append(t)
        return t

    def toks(self, n):
        return [self.tok() for _ in range(n)]

    def open(self):
        self.stacks.append(ExitStack())
        self.scope_toks.append([])

    def close(self):
        for t in self.scope_toks.pop():
            if t.w is not None:
                k, v = t.w
                if self.fence.get(k, 0) < v:
                    self.fence[k] = v
            for k, v in t.r.items():
                if self.fence.get(k, 0) < v:
                    self.fence[k] = v
        self.stacks.pop().close()

    def _deps(self, eng, reads, writes):
        deps = {}

        def add(k, v):
            if deps.get(k, 0) < v:
                deps[k] = v

        for t in reads:
            if t.w is not None:
                add(*t.w)
        for t in writes:
            if t.w is not None:
                add(*t.w)
            for k, v in t.r.items():
                if k == eng:
                    continue
                add(k, v)
        waits = []
        for k, v in deps.items():
            if k == eng and eng == "pe":
                continue
            if self.known[eng].get(k, 0) >= v:
                continue
            self.known[eng][k] = v
            waits.append((k, v))
        return waits

    def op(self, eng, fn, reads=(), writes=()):
        waits = self._deps(eng, reads, writes)
        self.cnt[eng] += 1
        mark = (eng, self.cnt[eng])
        self.prog[eng].append((waits, fn, mark))
        for t in reads:
            t.r[eng] = mark[1]
        for t in writes:
            t.w = mark
            t.r = {}
        return mark

    def dma(self, q, out, in_, reads=(), writes=(), **kw):
        idx = self.dma_n % self.NDMA
        self.dma_n += 1
        key = ("dma", idx)
        waits = self._deps(q, reads, writes)
        prev = self.dma_last[idx]
        if prev and self.known[q].get(key, 0) < prev:
            self.known[q][key] = prev
            waits.append((key, prev))
        val = prev + 16
        self.dma_last[idx] = val
        mark = (key, val)

        def fn(e, out=out, in_=in_, kw=kw):
            return e.dma_start(out=out, in_=in_, **kw)

        self.prog[q].append((waits, fn, mark))
        for t in reads:
            t.r[key] = val
        for t in writes:
            t.w = mark
            t.r = {}
        return mark

    def finish(self, toks):
        waits = [t.w for t in toks if t.w is not None]
        self.prog["sp"].append((waits, None, None))

    def emit(self):
        nc = self.nc
        es = self.stacks[0]
        sems = {}
        for e in self.ENG:
            sems[e] = es.enter_context(nc.semaphore("s_" + e))
        for i in range(self.NDMA):
            sems[("dma", i)] = es.enter_context(nc.semaphore("s_dma%d" % i))
        prog = self.prog

        def run(engname, e):
            for waits, fn, mark in prog[engname]:
                for k, v in waits:
                    e.wait_ge(sems[k], v)
                if fn is None:
                    continue
                inst = fn(e)
                if mark[0] == engname:
                    inst.then_inc(sems[engname], 1)
                else:
                    inst.then_inc(sems[mark[0]], 16)

        with nc.Block() as block:
            @block.tensor
            def _(e):
                run("pe", e)

            @block.scalar
            def _(e):
                run("act", e)

            @block.vector
            def _(e):
                run("dve", e)

            @block.gpsimd
            def _(e):
                run("pool", e)

            @block.sync
            def _(e):
                run("sp", e)
        es.close()


Q_A = [0, 1, 2, 3, 8, 9, 10, 11]
Q_B = [4, 5, 6, 7, 12, 13, 14, 15]

VEC = {}


def _vec_layout():
    c = 0
    for j in range(2):
        for nm, n in (("e_norm", 8), ("conv_w", 16), ("conv_b", 4), ("ga_b", 4), ("gx_b", 4), ("lam", 4),
                      ("q_norm", 2), ("kv_norm", 1)):
            VEC[(nm, j)] = c
            c += n
    for j in range(2):
        VEC[("o_norm", j)] = c
        c += 8
        VEC[("ikn", j)] = c
        c += 1
    for l in range(4):
        VEC[("m_norm", l)] = c
        c += 8
    VEC["final"] = c
    c += 8
    VEC["freq"] = c
    c += 1
    VEC["sgn"] = c
    c += 1
    return c


NVEC = _vec_layout()


def _cols(v):
    v = np.asarray(v, np.float32)
    return v.reshape(-1, 128).T


def prep_shared(inp):
    f = np.float32
    sh = {}
    vec = np.zeros((128, NVEC), f)
    for j in range(2):
        vec[:, VEC[("e_norm", j)]:VEC[("e_norm", j)] + 8] = _cols(inp["e_norm"][j])
        cw = inp["e_conv_w"][j]
        for t in range(4):
            vec[:, VEC[("conv_w", j)] + 4 * t:VEC[("conv_w", j)] + 4 * t + 4] = _cols(cw[t])
        vec[:, VEC[("conv_b", j)]:VEC[("conv_b", j)] + 4] = _cols(inp["e_conv_b"][j])
        vec[:, VEC[("ga_b", j)]:VEC[("ga_b", j)] + 4] = _cols(inp["e_ga_b"][j])
        vec[:, VEC[("gx_b", j)]:VEC[("gx_b", j)] + 4] = _cols(inp["e_gx_b"][j])
        vec[:, VEC[("lam", j)]:VEC[("lam", j)] + 4] = _cols(inp["e_lambda"][j])
        vec[:, VEC[("q_norm", j)]:VEC[("q_norm", j)] + 2] = _cols(inp["e_q_norm"][j])
        vec[:, VEC[("kv_norm", j)]:VEC[("kv_norm", j)] + 1] = _cols(inp["e_kv_norm"][j])
        vec[:, VEC[("o_norm", j)]:VEC[("o_norm", j)] + 8] = _cols(inp["o_norm"][j])
        g = np.asarray(inp["o_idx_k_norm"][j], f)
        vec[:, VEC[("ikn", j)]] = np.concatenate([g, g])
    for l in range(4):
        vec[:, VEC[("m_norm", l)]:VEC[("m_norm", l)] + 8] = _cols(inp["m_norm"][l])
    vec[:, VEC["final"]:VEC["final"] + 8] = _cols(inp["final_norm"])
    freqs = (10000.0 ** (-np.arange(0, 32, 2, dtype=np.float32) / 32)).astype(f)
    vec[64:96, VEC["freq"]] = np.concatenate([freqs, freqs])
    vec[64:80, VEC["sgn"]] = -1.0
    vec[80:96, VEC["sgn"]] = 1.0
    sh["vecs"] = vec

    e_w_in = np.asarray(inp["e_w_in"], f)
    sh["e_w_in"] = np.ascontiguousarray(e_w_in)
    kr = e_w_in[:, :, 1408:1440]
    junk = e_w_in[:, :, 1344:1408]
    sh["e_w_kr"] = np.ascontiguousarray(np.concatenate(
        [junk, kr, junk, kr[:, :, 16:32], kr[:, :, 0:16]], axis=2))
    wuq = np.asarray(inp["e_w_uq"], f).reshape(2, 256, 8, 96)
    sw = np.concatenate([wuq[..., :64], wuq[..., 80:96], wuq[..., 64:80]], axis=-1)
    sh["e_w_uq"] = np.ascontiguousarray(wuq.reshape(2, 256, 768))
    sh["e_w_uq_sw"] = np.ascontiguousarray(sw.reshape(2, 256, 768))
    wukv = np.asarray(inp["e_w_ukv"], f).reshape(2, 128, 8, 128)
    sh["e_w_uk"] = np.ascontiguousarray(wukv[..., :64].reshape(2, 128, 512))
    sh["e_w_uv"] = np.ascontiguousarray(wukv[..., 64:].reshape(2, 128, 512))
    for nm, src in (("e_ga_bd", "e_ga_w"), ("e_gx_bd", "e_gx_w")):
        w = np.asarray(inp[src], f)
        bd = np.zeros((2, 4, 128, 128), f)
        for c in range(4):
            bd[:, c, 0:64, 0:64] = w[:, 2 * c]
            bd[:, c, 64:128, 64:128] = w[:, 2 * c + 1]
        sh[nm] = bd
    sh["e_w_out"] = np.ascontiguousarray(np.asarray(inp["e_w_out"], f))

    o_w_in = np.asarray(inp["o_w_in"], f)
    qcols = []
    for j in range(8):
        qcols += list(range(Q_A[j] * 64, Q_A[j] * 64 + 64)) + list(range(Q_B[j] * 64, Q_B[j] * 64 + 64))
    ki = o_w_in[:, :, 2048:2112]
    sh["o_w_in"] = np.ascontiguousarray(np.concatenate(
        [o_w_in[:, :, qcols], o_w_in[:, :, 1024:2048], ki, ki, o_w_in[:, :, 2112:2120]], axis=2))
    sh["o_w_out"] = np.ascontiguousarray(np.asarray(inp["o_w_out"], f)[:, qcols, :])
    sh["m_w1"] = np.ascontiguousarray(np.asarray(inp["m_w1"], f))
    sh["m_w2"] = np.ascontiguousarray(np.asarray(inp["m_w2"], f))

    sh["c_ident"] = np.eye(128, dtype=f)
    mm = np.zeros((4, 128, 512), f)
    kk = np.arange(128)[:, None]
    qq = np.arange(512)[None, :]
    for j in range(4):
        mm[j] = np.where((128 * j + kk) // 64 <= qq // 64, 0.0, NEG)
    sh["c_maskM"] = mm
    cm = np.zeros((128, 128), f)
    cm[0:64, 64:128] = -1e30
    sh["c_cm"] = cm
    bdo = np.zeros((128, 128), f)
    bdo[0:64, 0:64] = 1.0
    bdo[64:128, 64:128] = 1.0
    sh["c_bd1"] = bdo
    sh["c_pow"] = np.tile((2.0 ** -(np.arange(NBIS, dtype=np.float32) + 1.0))[None, :], (128, 1)).astype(f)
    return sh


SHAPES = {
    "vecs": [128, NVEC], "e_w_in": [2, 1024, 1440], "e_w_kr": [2, 1024, 192], "e_w_uq": [2, 256, 768],
    "e_w_uq_sw": [2, 256, 768], "e_w_uk": [2, 128, 512], "e_w_uv": [2, 128, 512], "e_ga_bd": [2, 4, 128, 128],
    "e_gx_bd": [2, 4, 128, 128], "e_w_out": [2, 1024, 1024], "o_w_in": [2, 1024, 2184], "o_w_out": [2, 1024, 1024],
    "m_w1": [4, 1024, 4096], "m_w2": [4, 4096, 1024], "c_ident": [128, 128], "c_maskM": [4, 128, 512],
    "c_cm": [128, 128], "c_bd1": [128, 128], "c_pow": [128, NBIS],
}


def build(stages):
    nc = bass.Bass("TRN2", target_bir_lowering=False)
    mk = MK(nc)
    small = not any(st[0] == "m" for st in stages)
    dr = {k: nc.dram_tensor(k, ([1, 128, 128] if (small and k in ("m_w1", "m_w2")) else v), F32, kind="ExternalInput").ap() for k, v in SHAPES.items()}
    xin = nc.dram_tensor("xT", [D, S], F32, kind="ExternalInput").ap()
    posin = nc.dram_tensor("pos", [1, S], I32, kind="ExternalInput").ap()
    yout = nc.dram_tensor("yT", [D, S], F32, kind="ExternalOutput").ap()

    xT = mk.sb("xT_sb", [128, 8, S], F32)
    tx = [[mk.tok() for _ in range(NG)] for _ in range(8)]
    vecs = mk.sb("vecs_sb", [128, NVEC], F32)
    tvec = mk.tok()
    ident_b = mk.sb("ident_b", [128, 128], BF16)
    ident_f = mk.sb("ident_f", [128, 128], F32)
    ones_b = mk.sb("ones_b", [128, 128], BF16)
    bd1_b = mk.sb("bd1_b", [128, 128], BF16)
    ones_f = mk.sb("ones_f", [128, 64], F32)
    maskM = mk.sb("maskM", [128, 4, 512], BF16)
    cm_f = mk.sb("cm_f", [128, 128], F32)
    pow_f = mk.sb("pow_f", [128, NBIS], F32)
    tconst = mk.tok()
    cosT = mk.sb("cosT", [96, S], F32)
    sinT = mk.sb("sinT", [96, S], F32)
    ttab = mk.tok()

    NB = 5
    banks = [mk.ps("bank%d" % i, [128, 512], F32) for i in range(NB)]
    tbank = mk.toks(NB)
    obank = [mk.ps("obank%d" % i, [128, 512], F32) for i in range(2)]
    tobank = mk.toks(2)
    state = {"b": 0, "o": 0}

    def bank():
        i = state["b"] % NB
        state["b"] += 1
        return banks[i], tbank[i]

    def obk():
        i = state["o"] % 2
        state["o"] += 1
        return obank[i], tobank[i]

    def gs(g):
        return slice(g * GS, (g + 1) * GS)

    def V(key, n=1, off=0):
        c = (VEC[key] if not isinstance(key, int) else key) + off
        return vecs[:, c:c + n]

    for c in range(8):
        mk.dma("sp", xT[:, c, :], xin[c * 128:(c + 1) * 128, :], writes=tx[c])
    mk.dma("sp", vecs[:], dr["vecs"], writes=[tvec])
    mk.dma("sp", ident_f[:], dr["c_ident"], writes=[tconst])
    mk.dma("sp", cm_f[:], dr["c_cm"], writes=[tconst])
    mk.dma("sp", pow_f[:], dr["c_pow"], writes=[tconst])
    mk.dma("pool", ident_b[:], dr["c_ident"], writes=[tconst])
    mk.dma("pool", bd1_b[:], dr["c_bd1"], writes=[tconst])
    mk.dma("pool", maskM[:], dr["c_maskM"].rearrange("j p q -> p j q"), writes=[tconst])
    mk.op("dve", lambda e: e.memset(ones_b[:], 1.0), writes=[tconst])
    mk.op("dve", lambda e: e.memset(ones_f[:], 1.0), writes=[tconst])

    def rope_tables():
        mk.open()
        R = slice(64, 96)
        pos_i = mk.sb("pos_i", [96, S], I32)
        ang = mk.sb("ang", [96, S], F32)
        kf = mk.sb("kf", [96, S], F32)
        ki_ = mk.sb("ki_", [96, S], I32)
        r2 = mk.sb("r2", [96, S], F32)
        acc = mk.sb("acc", [96, S], F32)
        t = mk.tok()
        mk.dma("sp", pos_i[R, :], posin.broadcast_to([32, S]), writes=[t])
        mk.op("dve", lambda e: e.tensor_copy(out=ang[R, :], in_=pos_i[R, :]), reads=[t], writes=[t])
        mk.op("dve", lambda e: e.tensor_scalar(out=ang[R, :], in0=ang[R, :], scalar1=V("freq")[R, :], scalar2=None, op0=ALU.mult),
              reads=[t, tvec], writes=[t])
        TWO_PI = 2.0 * np.pi
        C1 = 6.28125
        C2 = float(np.float32(TWO_PI - 6.28125))
        C3 = float(TWO_PI - 6.28125 - np.float64(np.float32(TWO_PI - 6.28125)))
        for which in range(2):
            src = ang
            if which == 1:
                mk.op("dve", lambda e: e.tensor_scalar(out=r2[R, :], in0=ang[R, :], scalar1=float(np.pi / 2), scalar2=None, op0=ALU.add),
                      reads=[t], writes=[t])
                src = r2
            mk.op("dve", lambda e, src=src: e.tensor_scalar(out=kf[R, :], in0=src[R, :], scalar1=float(1.0 / TWO_PI), scalar2=None, op0=ALU.mult),
                  reads=[t], writes=[t])
            mk.op("dve", lambda e: e.tensor_copy(out=ki_[R, :], in_=kf[R, :]), reads=[t], writes=[t])
            mk.op("dve", lambda e: e.tensor_copy(out=kf[R, :], in_=ki_[R, :]), reads=[t], writes=[t])
            for ci, cc in enumerate((C1, C2, C3)):
                s_in = src if ci == 0 else acc
                mk.op("dve", lambda e, s_in=s_in, cc=cc: e.scalar_tensor_tensor(out=acc[R, :], in0=kf[R, :], scalar=float(-cc), in1=s_in[R, :],
                                                                              op0=ALU.mult, op1=ALU.add), reads=[t], writes=[t])
            mk.op("dve", lambda e: e.tensor_scalar(out=kf[R, :], in0=acc[R, :], scalar1=float(np.pi), scalar2=float(-TWO_PI), op0=ALU.is_gt, op1=ALU.mult),
                  reads=[t], writes=[t])
            mk.op("dve", lambda e: e.tensor_tensor(out=acc[R, :], in0=acc[R, :], in1=kf[R, :], op=ALU.add), reads=[t], writes=[t])
            mk.op("dve", lambda e: e.tensor_scalar(out=kf[R, :], in0=acc[R, :], scalar1=float(-np.pi), scalar2=float(TWO_PI), op0=ALU.is_lt, op1=ALU.mult),
                  reads=[t], writes=[t])
            mk.op("dve", lambda e: e.tensor_tensor(out=acc[R, :], in0=acc[R, :], in1=kf[R, :], op=ALU.add), reads=[t], writes=[t])
            mk.op("dve", lambda e: e.tensor_tensor(out=r2[R, :], in0=acc[R, :], in1=acc[R, :], op=ALU.mult), reads=[t], writes=[t])
            import math
            co = [(-1.0) ** k / math.factorial(2 * k + 1) for k in range(1, 9)]
            mk.op("dve", lambda e: e.tensor_scalar(out=kf[R, :], in0=r2[R, :], scalar1=float(co[-1]), scalar2=None, op0=ALU.mult),
                  reads=[t], writes=[t])
            for cc in reversed(co[:-1]):
                mk.op("dve", lambda e, cc=cc: e.scalar_tensor_tensor(out=kf[R, :], in0=kf[R, :], scalar=float(cc), in1=r2[R, :],
                                                                      op0=ALU.add, op1=ALU.mult), reads=[t], writes=[t])
            dst = sinT if which == 0 else cosT
            mk.op("dve", lambda e, dst=dst: e.scalar_tensor_tensor(out=dst[R, :], in0=kf[R, :], scalar=1.0, in1=acc[R, :],
                                                                  op0=ALU.add, op1=ALU.mult), reads=[t], writes=[t, ttab])
        mk.op("dve", lambda e: e.tensor_scalar(out=sinT[R, :], in0=sinT[R, :], scalar1=V("sgn")[R, :], scalar2=None, op0=ALU.mult),
              reads=[t, tvec, ttab], writes=[ttab])
        mk.close()

    def load_w(dst, src, tok):
        mk.dma("pool", dst, src, writes=[tok])

    def rstd_from(src_aps, src_toks, nfeat, ones_ap, sq, tsq, out_rstd, t_rstd, n=GS):
        ps, tp = bank()
        for i, (a, ta) in enumerate(zip(src_aps, src_toks)):
            mk.op("act", lambda e, a=a, i=i: e.activation(out=sq[:, i, 0:n], in_=a, func=AF.Square), reads=ta, writes=[tsq[i]])
        for i in range(len(src_aps)):
            mk.op("pe", lambda e, i=i: e.matmul(ps[:, 0:n], lhsT=ones_ap, rhs=sq[:, i, 0:n], start=(i == 0), stop=(i == len(src_aps) - 1)),
                  reads=[tsq[i], tconst], writes=[tp])
        mk.op("act", lambda e: e.activation(out=out_rstd[:, 0:n], in_=ps[:, 0:n], func=AF.Ln, scale=1.0 / nfeat, bias=EPS), reads=[tp], writes=[t_rstd])
        mk.op("act", lambda e: e.activation(out=out_rstd[:, 0:n], in_=out_rstd[:, 0:n], func=AF.Exp, scale=-0.5), reads=[t_rstd], writes=[t_rstd])

    def norm_group(g, gain_key, hT_g, th, sq, tsq, rstd, trs):
        rstd_from([xT[:, c, gs(g)] for c in range(8)], [[tx[c][g]] for c in range(8)], D, ones_b[:], sq, tsq, rstd, trs)
        for c in range(8):
            mk.op("dve", lambda e, c=c: e.scalar_tensor_tensor(out=hT_g[:, c, :], in0=xT[:, c, gs(g)], scalar=V(gain_key, 1, c), in1=rstd[:],
                                                               op0=ALU.mult, op1=ALU.mult), reads=[tx[c][g], trs, tvec], writes=[th[c]])

    def mlp(l):
        mk.open()
        hT = mk.sb("m_hT", [128, 8, S], BF16)
        th = [mk.toks(8) for _ in range(NG)]
        sq = mk.sb("m_sq", [128, 8, GS], BF16)
        tsq = mk.toks(8)
        rstd = mk.sb("m_rstd", [128, GS], F32)
        trs = mk.tok()
        w1s = [mk.sb("m_w1s%d" % i, [128, 8, 512], BF16) for i in range(2)]
        w2s = [mk.sb("m_w2s%d" % i, [128, 4, 1024], BF16) for i in range(2)]
        tw1 = mk.toks(2)
        tw2 = mk.toks(2)
        tmp = [mk.sb("m_tmp%d" % i, [128, GS], F32) for i in range(2)]
        ttmp = mk.toks(2)
        h1 = [mk.sb("m_h1%d" % i, [128, 4, GS], BF16) for i in range(2)]
        th1 = [mk.toks(4) for _ in range(2)]
        w1v = dr["m_w1"][l].rearrange("(kc p) n -> p kc n", p=128)
        w2v = dr["m_w2"][l].rearrange("(fc p) n -> p fc n", p=128)

        def loadslab(ffg):
            load_w(w1s[ffg % 2][:], w1v[:, :, ffg * 512:(ffg + 1) * 512], tw1[ffg % 2])
            load_w(w2s[ffg % 2][:], w2v[:, ffg * 4:(ffg + 1) * 4, :], tw2[ffg % 2])

        loadslab(0)
        for g in range(NG):
            class _H:
                pass
            norm_group(g, ("m_norm", l), hT[:, :, gs(g)], th[g], sq, tsq, rstd, trs)
        it = 0
        for ffg in range(8):
            if ffg + 1 < 8:
                loadslab(ffg + 1)
            a, b = w1s[ffg % 2], w2s[ffg % 2]
            for g in range(NG):
                hb = h1[it % 2]
                thb = th1[it % 2]
                it += 1
                for fc in range(4):
                    ps, tp = bank()
                    for kc in range(8):
                        mk.op("pe", lambda e, ps=ps, a=a, kc=kc, fc=fc, g=g: e.matmul(ps[:], lhsT=a[:, kc, fc * 128:(fc + 1) * 128], rhs=hT[:, kc, gs(g)],
                                                                                      start=(kc == 0), stop=(kc == 7)),
                              reads=[tw1[ffg % 2], th[g][kc]], writes=[tp])
                    tm, ttm = tmp[fc % 2], ttmp[fc % 2]
                    mk.op("act", lambda e, ps=ps, tm=tm: e.activation(out=tm[:], in_=ps[:], func=AF.Relu), reads=[tp], writes=[ttm])
                    mk.op("dve", lambda e, tm=tm, hb=hb, fc=fc: e.tensor_tensor(out=hb[:, fc, :], in0=tm[:], in1=tm[:], op=ALU.mult),
                          reads=[ttm], writes=[thb[fc]])
                for dc in range(8):
                    ps, tp = bank()
                    for fc in range(4):
                        mk.op("pe", lambda e, ps=ps, b=b, fc=fc, dc=dc, hb=hb: e.matmul(ps[:], lhsT=b[:, fc, dc * 128:(dc + 1) * 128], rhs=hb[:, fc, :],
                                                                                        start=(fc == 0), stop=(fc == 3)),
                              reads=[tw2[ffg % 2], thb[fc]], writes=[tp])
                    mk.op("dve", lambda e, ps=ps, dc=dc, g=g: e.tensor_tensor(out=xT[:, dc, gs(g)], in0=xT[:, dc, gs(g)], in1=ps[:], op=ALU.add),
                          reads=[tp, tx[dc][g]], writes=[tx[dc][g]])
        mk.close()

    def out_proj(wv, g, ymix, tym, wslab, twslab):
        for dc in range(8):
            ws, tws = wslab[dc % 2], twslab[dc % 2]
            load_w(ws[:], wv[:, :, dc * 128:(dc + 1) * 128], tws)
            ps, tp = bank()
            for kc in range(8):
                mk.op("pe", lambda e, ps=ps, ws=ws, kc=kc: e.matmul(ps[:], lhsT=ws[:, kc, :], rhs=ymix[:, kc, :], start=(kc == 0), stop=(kc == 7)),
                      reads=[tws, tym[kc]], writes=[tp])
            mk.op("dve", lambda e, ps=ps, dc=dc: e.tensor_tensor(out=xT[:, dc, gs(g)], in0=xT[:, dc, gs(g)], in1=ps[:], op=ALU.add),
                  reads=[tp, tx[dc][g]], writes=[tx[dc][g]])

    pend = []

    def flush_fin():
        while pend:
            pend.pop(0)()

    def attend(nchunks, k_ap_fn, q_ap, treads_k, treads_q, mask_fn, v_fn, tv, P, tP, pcount, ysb, tys, osb, tosb, rsb, trsb, after):
        LOOK = 2
        O, tO = obk()
        pidx = {}
        for c in range(nchunks + LOOK):
            if c < nchunks:
                ps, tp = bank()
                m = mask_fn(c)
                mk.op("pe", lambda e, ps=ps, c=c, m=m: e.matmul(ps[:], lhsT=k_ap_fn(c), rhs=q_ap, start=True, stop=(m is None)),
                      reads=treads_k(c) + treads_q, writes=[tp])
                if m is not None:
                    mk.op("pe", lambda e, ps=ps, m=m: e.matmul(ps[:], lhsT=ident_b[:], rhs=m[0], start=False, stop=True),
                          reads=m[1] + [tconst], writes=[tp])
                i = pcount[0] % len(P)
                pcount[0] += 1
                pidx[c] = i
                mk.op("act", lambda e, ps=ps, i=i: e.activation(out=P[i][:], in_=ps[:], func=AF.Exp), reads=[tp], writes=[tP[i]])
            if c == min(LOOK, nchunks) - 1:
                flush_fin()
            cc = c - LOOK
            if cc >= 0:
                i = pidx[cc]
                mk.op("pe", lambda e, O=O, cc=cc, i=i: e.matmul(O[0:65, :], lhsT=v_fn(cc), rhs=P[i][:], start=(cc == 0), stop=(cc == nchunks - 1)),
                      reads=[tP[i]] + tv(cc), writes=[tO])

        def fin():
            mk.op("dve", lambda e, O=O: e.reciprocal(out=rsb[64:65, 0:GS], in_=O[64:65, :]), reads=[tO], writes=[trsb])
            mk.op("act", lambda e, O=O: e.activation(out=osb[0:64, 0:GS], in_=O[0:64, :], func=AF.Copy), reads=[tO], writes=[tosb])
            ps, tp = bank()
            mk.op("pe", lambda e, ps=ps: e.matmul(ps[0:64, :], lhsT=ones_f[64:65, 0:64], rhs=rsb[64:65, 0:GS], start=True, stop=True),
                  reads=[trsb, tconst], writes=[tp])
            mk.op("dve", lambda e, ps=ps: e.tensor_tensor(out=ysb[0:64, :], in0=osb[0:64, 0:GS], in1=ps[0:64, :], op=ALU.mult),
                  reads=[tosb, tp], writes=[tys])
            after()

        pend.append(fin)

    def even_mixer(j):
        mk.open()
        SC_Q = 96.0 ** -0.5
        hT = mk.sb("e_hT", [128, 8, GS], BF16)
        th = mk.toks(8)
        sq = mk.sb("e_sq", [128, 8, GS], BF16)
        tsq = mk.toks(8)
        rstd = mk.sb("e_rstd", [128, GS], F32)
        trs = mk.tok()
        wsl = [mk.sb("e_wsl%d" % i, [128, 8, 128], BF16) for i in range(2)]
        twsl = mk.toks(2)
        wkr = mk.sb("e_wkr", [128, 8, 192], BF16)
        twkr = mk.tok()
        wuq = mk.sb("e_wuq", [128, 2, 768], BF16)
        wuqs = mk.sb("e_wuqs", [128, 2, 768], BF16)
        wuk = mk.sb("e_wuk", [128, 512], BF16)
        wuv = mk.sb("e_wuv", [128, 512], BF16)
        gabd = mk.sb("e_gabd", [128, 4, 128], BF16)
        gxbd = mk.sb("e_gxbd", [128, 4, 128], BF16)
        tw = mk.tok()
        wosl = [mk.sb("e_wosl%d" % i, [128, 8, 128], BF16) for i in range(2)]
        twosl = mk.toks(2)
        clam = mk.sb("e_clam", [128, 4], F32)
        tclam = mk.tok()
        ymix = mk.sb("e_ymix", [128, 8, GS], BF16)
        tym = mk.toks(8)
        XR = mk.sb("e_XR", [128, GS + 3], F32)
        halo = mk.sb("e_halo", [128, 4, 3], F32)
        thalo = mk.toks(4)
        hlast = mk.sb("e_hlast", [128, 4], F32)
        thl = mk.toks(4)
        XC = mk.sb("e_XC", [128, GS], F32)
        XCb = mk.sb("e_XCb", [128, GS], BF16)
        GG = mk.sb("e_GG", [128, GS], F32)
        TA = mk.sb("e_TA", [128, GS], F32)
        TI = mk.sb("e_TI", [128, GS], F32)
        TM = mk.sb("e_TM", [128, GS], F32)
        HH = mk.sb("e_HH", [128, GS], F32)
        tXR, tXC, tXCb, tGG, tTA, tTI, tTM, tHH = mk.toks(8)
        cqn = mk.sb("e_cqn", [128, 2, GS], BF16)
        tcqn = mk.toks(2)
        ckvn = mk.sb("e_ckvn", [128, S], BF16)
        tckvn = mk.toks(NG)
        KR = mk.sb("e_KR", [96, S], BF16)
        tKR = mk.toks(NG)
        Vg = mk.sb("e_V", [128, 16, 8, 65], BF16)
        tV = mk.toks(16)
        kT = [mk.sb("e_kT%d" % i, [96, S], BF16) for i in range(2)]
        tkT = [mk.toks(NG) for _ in range(2)]
        qT = [mk.sb("e_qT%d" % i, [96, GS], BF16) for i in range(2)]
        tqT = mk.toks(2)
        P = [mk.sb("e_P%d" % i, [128, GS], BF16) for i in range(3)]
        tP = mk.toks(3)
        pcount = [0]
        ysb = [mk.sb("e_ysb%d" % i, [64, GS], BF16) for i in range(2)]
        tys = mk.toks(2)
        rsb = mk.sb("e_rsb", [65, GS], F32)
        trsb = mk.tok()
        R = slice(64, 96)
        CQl = [TA, TI]
        tCQ = [tTA, tTI]
        CKV, tCKV = TA, tTA
        KA, tKA, KB, tKB = TM, tTM, HH, tHH
        QA, tQA, QB, tQB = XC, tXC, GG, tGG
        osb, tosb = XR, tXR

        load_w(wuq[:], dr["e_w_uq"][j].rearrange("(kc p) n -> p kc n", p=128), tw)
        load_w(wuqs[:], dr["e_w_uq_sw"][j].rearrange("(kc p) n -> p kc n", p=128), tw)
        load_w(wuk[:], dr["e_w_uk"][j], tw)
        load_w(wuv[:], dr["e_w_uv"][j], tw)
        load_w(gabd[:], dr["e_ga_bd"][j].rearrange("c p n -> p c n"), tw)
        load_w(gxbd[:], dr["e_gx_bd"][j].rearrange("c p n -> p c n"), tw)
        load_w(wkr[:], dr["e_w_kr"][j].rearrange("(kc p) n -> p kc n", p=128), twkr)
        mk.op("pool", lambda e: e.memset(Vg[:, :, :, 64:65], 1.0), writes=tV)
        mk.op("act", lambda e: e.activation(out=clam[:], in_=V(("lam", j), 4), func=AF.Exp, scale=-1.0), reads=[tvec], writes=[tclam])
        mk.op("act", lambda e: e.activation(out=clam[:], in_=clam[:], func=AF.Ln, bias=1.0), reads=[tclam], writes=[tclam])
        mk.op("dve", lambda e: e.tensor_scalar(out=clam[:], in0=clam[:], scalar1=-8.0, scalar2=None, op0=ALU.mult), reads=[tclam], writes=[tclam])

        w_in = dr["e_w_in"][j].rearrange("(kc p) n -> p kc n", p=128)
        wov = dr["e_w_out"][j].rearrange("(kc p) n -> p kc n", p=128)
        slab_i = [0]

        def proj(col0, ncol, g, lhs_slab=None, tslab=None):
            if lhs_slab is None:
                i = slab_i[0] % 2
                slab_i[0] += 1
                lhs_slab, tslab = wsl[i], twsl[i]
                load_w(lhs_slab[:, :, 0:ncol], w_in[:, :, col0:col0 + ncol], tslab)
                c0 = 0
            else:
                c0 = col0
            ps, tp = bank()
            for kc in range(8):
                mk.op("pe", lambda e, ps=ps, kc=kc, c0=c0, s=lhs_slab: e.matmul(ps[0:ncol, :], lhsT=s[:, kc, c0:c0 + ncol], rhs=hT[:, kc, :],
                                                                                start=(kc == 0), stop=(kc == 7)),
                      reads=[tslab, th[kc]], writes=[tp])
            return ps, tp

        for g in range(NG):
            norm_group(g, ("e_norm", j), hT, th, sq, tsq, rstd, trs)
            for c in range(4):
                ps, tp = proj(c * 128, 128, g)
                if g == 0:
                    mk.op("dve", lambda e: e.memset(XR[:, 0:3], 0.0), writes=[tXR])
                else:
                    mk.op("dve", lambda e, c=c: e.tensor_copy(out=XR[:, 0:3], in_=halo[:, c, :]), reads=[thalo[c]], writes=[tXR])
                mk.op("act", lambda e, ps=ps: e.activation(out=XR[:, 3:GS + 3], in_=ps[:], func=AF.Copy), reads=[tp], writes=[tXR])
                mk.op("dve", lambda e, c=c: e.tensor_copy(out=halo[:, c, :], in_=XR[:, GS:GS + 3]), reads=[tXR], writes=[thalo[c]])
                ps2, tp2 = proj(512 + c * 128, 128, g)
                mk.op("act", lambda e, ps2=ps2: e.activation(out=GG[:], in_=ps2[:], func=AF.Gelu_apprx_tanh), reads=[tp2], writes=[tGG])
                cw = VEC[("conv_w", j)]
                mk.op("dve", lambda e, c=c: e.tensor_scalar(out=XC[:], in0=XR[:, 3:GS + 3], scalar1=V(cw + 12 + c), scalar2=V(("conv_b", j), 1, c),
                                                            op0=ALU.mult, op1=ALU.add), reads=[tXR, tvec], writes=[tXC])
                for t in range(3):
                    mk.op("dve", lambda e, c=c, t=t: e.scalar_tensor_tensor(out=XC[:], in0=XR[:, t:GS + t], scalar=V(cw + 4 * t + c), in1=XC[:],
                                                                            op0=ALU.mult, op1=ALU.add), reads=[tXR, tvec], writes=[tXC])
                mk.op("act", lambda e: e.activation(out=XCb[:], in_=XC[:], func=AF.Copy), reads=[tXC], writes=[tXCb])
                psa, tpa = bank()
                mk.op("pe", lambda e, psa=psa, c=c: e.matmul(psa[:], lhsT=gabd[:, c, :], rhs=XCb[:], start=True, stop=True), reads=[tw, tXCb], writes=[tpa])
                psx, tpx = bank()
                mk.op("pe", lambda e, psx=psx, c=c: e.matmul(psx[:], lhsT=gxbd[:, c, :], rhs=XCb[:], start=True, stop=True), reads=[tw, tXCb], writes=[tpx])
                mk.op("act", lambda e, psa=psa, c=c: e.activation(out=TA[:], in_=psa[:], func=AF.Sigmoid, bias=V(("ga_b", j), 1, c)), reads=[tpa, tvec], writes=[tTA])
                mk.op("act", lambda e, psx=psx, c=c: e.activation(out=TI[:], in_=psx[:], func=AF.Sigmoid, bias=V(("gx_b", j), 1, c)), reads=[tpx, tvec], writes=[tTI])
                mk.op("act", lambda e, c=c: e.activation(out=TA[:], in_=TA[:], func=AF.Exp, scale=clam[:, c:c + 1]), reads=[tTA, tclam], writes=[tTA])
                mk.op("act", lambda e: e.activation(out=TM[:], in_=TA[:], func=AF.Square), reads=[tTA], writes=[tTM])
                mk.op("act", lambda e: e.activation(out=TM[:], in_=TM[:], func=AF.Ln, scale=-1.0, bias=1.0), reads=[tTM], writes=[tTM])
                mk.op("act", lambda e: e.activation(out=TM[:], in_=TM[:], func=AF.Exp, scale=0.5), reads=[tTM], writes=[tTM])
                mk.op("dve", lambda e: e.tensor_tensor(out=TI[:], in0=TI[:], in1=XC[:], op=ALU.mult), reads=[tTI, tXC], writes=[tTI])
                mk.op("dve", lambda e: e.tensor_tensor(out=TI[:], in0=TI[:], in1=TM[:], op=ALU.mult), reads=[tTI, tTM], writes=[tTI])
                init = 0.0 if g == 0 else hlast[:, c:c + 1]
                mk.op("dve", lambda e, init=init: e.tensor_tensor_scan(out=HH[:], data0=TA[:], data1=TI[:], initial=init, op0=ALU.mult, op1=ALU.add),
                      reads=[tTA, tTI, thl[c]], writes=[tHH])
                mk.op("dve", lambda e, c=c: e.tensor_copy(out=hlast[:, c:c + 1], in_=HH[:, GS - 1:GS]), reads=[tHH], writes=[thl[c]])
                mk.op("dve", lambda e, c=c: e.tensor_tensor(out=ymix[:, c, :], in0=HH[:], in1=GG[:], op=ALU.mult), reads=[tHH, tGG], writes=[tym[c]])
            for q in range(2):
                ps, tp = proj(1024 + q * 128, 128, g)
                mk.op("act", lambda e, ps=ps, q=q: e.activation(out=CQl[q][:], in_=ps[:], func=AF.Copy), reads=[tp], writes=[tCQ[q]])
            rstd_from([CQl[0][:], CQl[1][:]], [[tCQ[0]], [tCQ[1]]], 256, ones_b[:], sq, tsq, rstd, trs)
            for q in range(2):
                mk.op("dve", lambda e, q=q: e.scalar_tensor_tensor(out=cqn[:, q, :], in0=CQl[q][:], scalar=V(("q_norm", j), 1, q), in1=rstd[:],
                                                                   op0=ALU.mult, op1=ALU.mult), reads=[tCQ[q], trs, tvec], writes=[tcqn[q]])
            ps, tp = proj(1280, 128, g)
            mk.op("act", lambda e, ps=ps: e.activation(out=CKV[:], in_=ps[:], func=AF.Copy), reads=[tp], writes=[tCKV])
            rstd_from([CKV[:]], [[tCKV]], 128, ones_b[:], sq, tsq, rstd, trs)
            mk.op("dve", lambda e, g=g: e.scalar_tensor_tensor(out=ckvn[:, gs(g)], in0=CKV[:], scalar=V(("kv_norm", j)), in1=rstd[:],
                                                               op0=ALU.mult, op1=ALU.mult), reads=[tCKV, trs, tvec], writes=[tckvn[g]])
            psA, tpA = proj(0, 96, g, wkr, twkr)
            psB, tpB = proj(96, 96, g, wkr, twkr)
            mk.op("dve", lambda e, psA=psA, g=g: e.tensor_tensor(out=KA[R, :], in0=psA[R, :], in1=cosT[R, gs(g)], op=ALU.mult), reads=[tpA, ttab], writes=[tKA])
            mk.op("dve", lambda e, psB=psB, g=g: e.tensor_tensor(out=KB[R, :], in0=psB[R, :], in1=sinT[R, gs(g)], op=ALU.mult), reads=[tpB, ttab], writes=[tKB])
            mk.op("dve", lambda e, g=g: e.tensor_tensor(out=KR[R, gs(g)], in0=KA[R, :], in1=KB[R, :], op=ALU.add), reads=[tKA, tKB], writes=[tKR[g]])
            for tt in range(4 * g, 4 * g + 4):
                ps, tp = bank()
                mk.op("pe", lambda e, ps=ps, tt=tt: e.matmul(ps[:], lhsT=ckvn[:, tt * 128:(tt + 1) * 128], rhs=wuv[:], start=True, stop=True),
                      reads=[tckvn[g], tw], writes=[tp])
                mk.op("act", lambda e, ps=ps, tt=tt: e.activation(out=Vg[:, tt, :, 0:64], in_=ps[:].rearrange("p (h d) -> p h d", h=8), func=AF.Copy),
                      reads=[tp], writes=[tV[tt]])
            nch = 4 * (g + 1)
            for h in range(8):
                kb, tkb = kT[h % 2], tkT[h % 2]
                for g2 in range(g + 1):
                    ps, tp = bank()
                    mk.op("pe", lambda e, ps=ps, g2=g2, h=h: e.matmul(ps[0:64, :], lhsT=wuk[:, h * 64:(h + 1) * 64], rhs=ckvn[:, gs(g2)], start=True, stop=True),
                          reads=[tw, tckvn[g2]], writes=[tp])
                    mk.op("act", lambda e, ps=ps, g2=g2, kb=kb: e.activation(out=kb[0:64, gs(g2)], in_=ps[0:64, :], func=AF.Copy), reads=[tp], writes=[tkb[g2]])
                    mk.op("dve", lambda e, g2=g2, kb=kb: e.tensor_copy(out=kb[R, gs(g2)], in_=KR[R, gs(g2)]), reads=[tKR[g2]], writes=[tkb[g2]])
                qb, tqb = qT[h % 2], tqT[h % 2]
                psA, tpA = bank()
                psB, tpB = bank()
                for q in range(2):
                    mk.op("pe", lambda e, psA=psA, q=q, h=h: e.matmul(psA[0:96, :], lhsT=wuq[:, q, h * 96:(h + 1) * 96], rhs=cqn[:, q, :], start=(q == 0), stop=(q == 1)),
                          reads=[tw, tcqn[q]], writes=[tpA])
                for q in range(2):
                    mk.op("pe", lambda e, psB=psB, q=q, h=h: e.matmul(psB[0:96, :], lhsT=wuqs[:, q, h * 96:(h + 1) * 96], rhs=cqn[:, q, :], start=(q == 0), stop=(q == 1)),
                          reads=[tw, tcqn[q]], writes=[tpB])
                mk.op("act", lambda e, psA=psA, qb=qb: e.activation(out=qb[0:64, :], in_=psA[0:64, :], func=AF.Copy, scale=SC_Q), reads=[tpA], writes=[tqb])
                mk.op("dve", lambda e, psA=psA, g=g: e.tensor_tensor(out=QA[R, :], in0=psA[R, :], in1=cosT[R, gs(g)], op=ALU.mult), reads=[tpA, ttab], writes=[tQA])
                mk.op("dve", lambda e, psB=psB, g=g: e.tensor_tensor(out=QB[R, :], in0=psB[R, :], in1=sinT[R, gs(g)], op=ALU.mult), reads=[tpB, ttab], writes=[tQB])
                mk.op("dve", lambda e: e.tensor_scalar(out=QB[R, :], in0=QB[R, :], scalar1=SC_Q, scalar2=None, op0=ALU.mult), reads=[tQB], writes=[tQB])
                mk.op("dve", lambda e, qb=qb: e.scalar_tensor_tensor(out=qb[R, :], in0=QA[R, :], scalar=SC_Q, in1=QB[R, :], op0=ALU.mult, op1=ALU.add),
                      reads=[tQA, tQB], writes=[tqb])
                ys, tysi = ysb[h % 2], tys[h % 2]
                attend(nch,
                       lambda c, kb=kb: kb[:, c * 128:(c + 1) * 128],
                       qb[:], lambda c, tkb=tkb: [tkb[c // 4]], [tqb],
                       lambda c, g=g: ((maskM[:, c - 4 * g, :], [tconst]) if c >= 4 * g else None),
                       lambda c, h=h: Vg[:, c, h, :], lambda c: [tV[c]],
                       P, tP, pcount, ys, tysi, osb, tosb, rsb, trsb,
                       lambda h=h, ys=ys, tysi=tysi: mk.dma("sp", ymix[(h % 2) * 64:(h % 2) * 64 + 64, 4 + h // 2, :], ys[0:64, :],
                                                            reads=[tysi], writes=[tym[4 + h // 2]]))
            flush_fin()
            out_proj(wov, g, ymix, tym, wosl, twosl)
        mk.close()

    def odd_mixer(j):
        mk.open()
        SC_Q = 64.0 ** -0.5
        hT = mk.sb("o_hT", [128, 8, GS], BF16)
        th = mk.toks(8)
        sq = mk.sb("o_sq", [128, 8, GS], BF16)
        tsq = mk.toks(8)
        rstd = mk.sb("o_rstd", [128, GS], F32)
        trs = mk.tok()
        wsl = [mk.sb("o_wsl%d" % i, [128, 8, 256], BF16) for i in range(2)]
        twsl = mk.toks(2)
        wosl = [mk.sb("o_wosl%d" % i, [128, 8, 128], BF16) for i in range(2)]
        twosl = mk.toks(2)
        ymix = mk.sb("o_ymix", [128, 8, GS], BF16)
        tym = mk.toks(8)
        Kt = mk.sb("o_K", [128, 2, S], BF16)
        tK = [mk.toks(NG) for _ in range(2)]
        Vg = mk.sb("o_V", [128, 16, 4, 65], BF16)
        tV = mk.toks(16)
        KI = mk.sb("o_KI", [128, GS], F32)
        tKI = mk.tok()
        kin = mk.sb("o_kin", [128, S], BF16)
        tkin = mk.toks(NG)
        qg = mk.sb("o_qg", [128, 8, GS], BF16)
        tqg = mk.toks(8)
        qig = mk.sb("o_qig", [128, 4, GS], BF16)
        tqig = mk.toks(4)
        WI = mk.sb("o_WI", [128, 4, 8], F32)
        WA = mk.sb("o_WA", [128, 4, 8], F32)
        WS = mk.sb("o_WS", [128, 4, 8], F32)
        tWI = mk.toks(4)
        maskT = mk.sb("o_maskT", [128, 16, GS], BF16)
        tmT = [mk.toks(4) for _ in range(16)]
        Sidx = [mk.sb("o_Sidx0", [128, S], F32)] * 2
        tSx = [mk.tok()] * 2
        rl = [mk.sb("o_rl%d" % i, [128, GS], F32) for i in range(2)]
        trl = mk.toks(2)
        mb = mk.sb("o_mb", [128, S], BF16)
        tmb = mk.tok()
        bs = mk.sb("o_bs", [128, 8], F32)
        tbs = mk.tok()
        Wb = mk.sb("o_Wb", [128, NBIS], F32)
        ptr = mk.ps("o_ptr", [128, 1024], BF16)
        tptr = mk.tok()
        P = [mk.sb("o_P%d" % i, [128, GS], BF16) for i in range(3)]
        tP = mk.toks(3)
        pcount = [0]
        ysb = [mk.sb("o_ysb%d" % i, [64, GS], BF16) for i in range(2)]
        tys = mk.toks(2)
        rsb = mk.sb("o_rsb", [65, GS], F32)
        trsb = mk.tok()

        junk, tjunk = mb, tmb
        osb, tosb = KI, tKI
        w_in = dr["o_w_in"][j].rearrange("(kc p) n -> p kc n", p=128)
        wov = dr["o_w_out"][j].rearrange("(kc p) n -> p kc n", p=128)
        slab_i = [0]
        mk.op("pool", lambda e: e.memset(Vg[:, :, :, 64:65], 1.0), writes=tV)

        def slab(col0, ncol):
            i = slab_i[0] % 2
            slab_i[0] += 1
            load_w(wsl[i][:, :, 0:ncol], w_in[:, :, col0:col0 + ncol], twsl[i])
            return wsl[i], twsl[i]

        def projf(s, ts, c0, g_unused=None):
            ps, tp = bank()
            for kc in range(8):
                mk.op("pe", lambda e, ps=ps, kc=kc: e.matmul(ps[:], lhsT=s[:, kc, c0:c0 + 128], rhs=hT[:, kc, :], start=(kc == 0), stop=(kc == 7)),
                      reads=[ts, th[kc]], writes=[tp])
            return ps, tp

        for g in range(NG):
            norm_group(g, ("o_norm", j), hT, th, sq, tsq, rstd, trs)
            s, ts = slab(1024, 256)
            for m in range(2):
                ps, tp = projf(s, ts, m * 128)
                mk.op("act", lambda e, ps=ps, m=m, g=g: e.activation(out=Kt[:, m, gs(g)], in_=ps[:], func=AF.Copy), reads=[tp], writes=[tK[m][g]])
            s, ts = slab(1280, 256)
            for tt in range(4):
                ps, tp = bank()
                for kc in range(8):
                    mk.op("pe", lambda e, ps=ps, kc=kc, tt=tt, s=s: e.matmul(ps[:, 0:256], lhsT=hT[:, kc, tt * 128:(tt + 1) * 128], rhs=s[:, kc, 0:256],
                                                                             start=(kc == 0), stop=(kc == 7)), reads=[ts, th[kc]], writes=[tp])
                mk.op("act", lambda e, ps=ps, tt=tt, g=g: e.activation(out=Vg[:, 4 * g + tt, :, 0:64], in_=ps[:, 0:256].rearrange("p (h d) -> p h d", h=4), func=AF.Copy),
                      reads=[tp], writes=[tV[4 * g + tt]])
            s, ts = slab(2048, 136)
            ps, tp = projf(s, ts, 0)
            mk.op("act", lambda e, ps=ps: e.activation(out=KI[:], in_=ps[:], func=AF.Copy), reads=[tp], writes=[tKI])
            rstd_from([KI[:]], [[tKI]], 64, bd1_b[:], sq, tsq, rstd, trs)
            mk.op("dve", lambda e, g=g: e.scalar_tensor_tensor(out=kin[:, gs(g)], in0=KI[:], scalar=V(("ikn", j)), in1=rstd[:], op0=ALU.mult, op1=ALU.mult),
                  reads=[tKI, trs, tvec], writes=[tkin[g]])
            for tt in range(4):
                ps, tp = bank()
                for kc in range(8):
                    mk.op("pe", lambda e, ps=ps, kc=kc, tt=tt, s=s: e.matmul(ps[:, 0:8], lhsT=hT[:, kc, tt * 128:(tt + 1) * 128], rhs=s[:, kc, 128:136],
                                                                             start=(kc == 0), stop=(kc == 7)), reads=[ts, th[kc]], writes=[tp])
                mk.op("act", lambda e, ps=ps, tt=tt: e.activation(out=WI[:, tt, :], in_=ps[:, 0:8], func=AF.Copy), reads=[tp], writes=[tWI[tt]])
                mk.op("act", lambda e, tt=tt: e.activation(out=WA[:, tt, :], in_=WI[:, tt, :], func=AF.Abs), reads=[tWI[tt]], writes=[tWI[tt]])
                mk.op("act", lambda e, tt=tt: e.activation(out=WS[:, tt, :], in_=WI[:, tt, :], func=AF.Sign), reads=[tWI[tt]], writes=[tWI[tt]])
            for half in range(4):
                s, ts = slab(half * 256, 256)
                for m in range(2):
                    ps, tp = projf(s, ts, m * 128)
                    jj = 2 * half + m
                    mk.op("act", lambda e, ps=ps, jj=jj: e.activation(out=qg[:, jj, :], in_=ps[:], func=AF.Copy, scale=SC_Q), reads=[tp], writes=[tqg[jj]])
            for half in range(2):
                s, ts = slab(1536 + half * 256, 256)
                for m in range(2):
                    ps, tp = projf(s, ts, m * 128)
                    jj = 2 * half + m
                    mk.op("act", lambda e, ps=ps, jj=jj: e.activation(out=qig[:, jj, :], in_=ps[:], func=AF.Copy), reads=[tp], writes=[tqig[jj]])
            nch = 4 * (g + 1)
            for tt in range(4):
                i = 4 * g + tt
                qs = slice(tt * 128, (tt + 1) * 128)
                if i < 2:
                    for c in range(nch):
                        mk.op("pool", lambda e, c=c, qs=qs: e.tensor_copy(out=maskT[:, c, qs], in_=maskM[:, c, qs]), reads=[tconst], writes=[tmT[c][tt]])
                    continue
                kl = 128 * (i + 1)
                Sx, tS = Sidx[i % 2], tSx[i % 2]
                for kb in range((kl + 511) // 512):
                    n = min(512, kl - 512 * kb)
                    ks = slice(512 * kb, 512 * kb + n)
                    for hh in range(8):
                        hp = slice((hh % 2) * 64, (hh % 2) * 64 + 64)
                        ps, tp = bank()
                        mk.op("pe", lambda e, ps=ps, hh=hh, hp=hp, qs=qs, ks=ks, n=n: e.matmul(ps[:, 0:n], lhsT=qig[hp, hh // 2, qs], rhs=kin[hp, ks], start=True, stop=True),
                              reads=[tqig[hh // 2]] + [tkin[x] for x in range(kb, kb + 1)], writes=[tp])
                        r_, tr_ = rl[hh % 2], trl[hh % 2]
                        mk.op("act", lambda e, ps=ps, r_=r_, tt=tt, hh=hh, n=n: e.activation(out=r_[:, 0:n], in_=ps[:, 0:n], func=AF.Relu, scale=WA[:, tt, hh:hh + 1]),
                              reads=[tp, tWI[tt]], writes=[tr_])
                        if hh == 0:
                            mk.op("dve", lambda e, r_=r_, Sx=Sx, ks=ks, tt=tt, n=n: e.tensor_scalar(out=Sx[:, ks], in0=r_[:, 0:n], scalar1=WS[:, tt, 0:1], scalar2=None, op0=ALU.mult),
                                  reads=[tr_, tWI[tt]], writes=[tS])
                        else:
                            mk.op("dve", lambda e, r_=r_, Sx=Sx, ks=ks, tt=tt, hh=hh, n=n: e.scalar_tensor_tensor(out=Sx[:, ks], in0=r_[:, 0:n], scalar=WS[:, tt, hh:hh + 1], in1=Sx[:, ks],
                                                                                                               op0=ALU.mult, op1=ALU.add), reads=[tr_, tWI[tt]], writes=[tS])
                SV = Sx[:, 0:kl]
                mk.op("dve", lambda e, SV=SV: e.tensor_reduce(out=bs[:, 0:1], in_=SV, axis=AX.X, op=ALU.max), reads=[tS], writes=[tbs])
                mk.op("dve", lambda e, SV=SV: e.tensor_reduce(out=bs[:, 1:2], in_=SV, axis=AX.X, op=ALU.min), reads=[tS], writes=[tbs])
                mk.op("dve", lambda e, Sx=Sx, kl=kl: e.tensor_tensor(out=Sx[:, kl - 128:kl], in0=Sx[:, kl - 128:kl], in1=cm_f[:], op=ALU.add), reads=[tS, tconst], writes=[tS])
                mk.op("dve", lambda e: e.tensor_tensor(out=bs[:, 2:3], in0=bs[:, 0:1], in1=bs[:, 1:2], op=ALU.subtract), reads=[tbs], writes=[tbs])
                mk.op("dve", lambda e: e.tensor_scalar(out=Wb[:], in0=pow_f[:], scalar1=bs[:, 2:3], scalar2=None, op0=ALU.mult), reads=[tbs, tconst], writes=[tbs])
                mk.op("dve", lambda e: e.scalar_tensor_tensor(out=bs[:, 3:4], in0=bs[:, 2:3], scalar=0.5, in1=bs[:, 1:2], op0=ALU.mult, op1=ALU.add), reads=[tbs], writes=[tbs])
                for it in range(NBIS):
                    mk.op("dve", lambda e, SV=SV, kl=kl: e.tensor_scalar(out=junk[:, 0:kl], in0=SV, scalar1=bs[:, 3:4], scalar2=0.0, op0=ALU.is_gt, op1=ALU.add, accum_out=bs[:, 4:5]),
                          reads=[tS, tbs], writes=[tjunk, tbs])
                    mk.op("dve", lambda e: e.tensor_scalar(out=bs[:, 5:6], in0=bs[:, 4:5], scalar1=float(TOPK) - 0.5, scalar2=0.5, op0=ALU.is_gt, op1=ALU.subtract),
                          reads=[tbs], writes=[tbs])
                    mk.op("dve", lambda e, it=it: e.scalar_tensor_tensor(out=bs[:, 3:4], in0=bs[:, 5:6], scalar=Wb[:, it:it + 1], in1=bs[:, 3:4], op0=ALU.mult, op1=ALU.add),
                          reads=[tbs], writes=[tbs])
                mk.op("dve", lambda e, SV=SV, kl=kl: e.tensor_scalar(out=mb[:, 0:kl], in0=SV, scalar1=bs[:, 3:4], scalar2=NEG, op0=ALU.is_le, op1=ALU.mult),
                      reads=[tS, tbs], writes=[tmb])
                for c0 in range(0, i + 1, 8):
                    c1 = min(i + 1, c0 + 8)
                    for c in range(c0, c1):
                        mk.op("pe", lambda e, c=c, c0=c0: e.transpose(out=ptr[:, (c - c0) * 128:(c - c0 + 1) * 128], in_=mb[:, c * 128:(c + 1) * 128], identity=ident_b[:]),
                              reads=[tmb, tconst], writes=[tptr])
                    mk.op("act", lambda e, c0=c0, c1=c1, qs=qs: e.activation(out=maskT[:, c0:c1, qs], in_=ptr[:, 0:(c1 - c0) * 128].rearrange("p (c q) -> p c q", q=128), func=AF.Copy),
                          reads=[tptr], writes=[tmT[c][tt] for c in range(c0, c1)])
                for c in range(i + 1, nch):
                    mk.op("pool", lambda e, c=c, qs=qs: e.memset(maskT[:, c, qs], NEG), writes=[tmT[c][tt]])
            for jj in range(8):
                for half in range(2):
                    hp = slice(half * 64, half * 64 + 64)
                    kvh = 2 * (jj // 4) + half
                    hidx = 2 * jj + half
                    ys, tysi = ysb[hidx % 2], tys[hidx % 2]
                    attend(nch,
                           lambda c, hp=hp, jj=jj: Kt[hp, jj // 4, c * 128:(c + 1) * 128],
                           qg[hp, jj, :], lambda c, jj=jj: [tK[jj // 4][c // 4]], [tqg[jj]],
                           lambda c: (maskT[:, c, :], tmT[c]),
                           lambda c, kvh=kvh: Vg[:, c, kvh, :], lambda c: [tV[c]],
                           P, tP, pcount, ys, tysi, osb, tosb, rsb, trsb,
                           lambda hp=hp, jj=jj, ys=ys, tysi=tysi: mk.dma("sp", ymix[hp, jj, :], ys[0:64, :], reads=[tysi], writes=[tym[jj]]))
            flush_fin()
            out_proj(wov, g, ymix, tym, wosl, twosl)
        mk.close()

    def final_out(do_norm):
        mk.open()
        sq = mk.sb("f_sq", [128, 8, GS], BF16)
        tsq = mk.toks(8)
        rstd = mk.sb("f_rstd", [128, GS], F32)
        trs = mk.tok()
        ob = [mk.sb("f_ob%d" % i, [128, GS], F32) for i in range(2)]
        tob = mk.toks(2)
        touts = []
        n = 0
        for g in range(NG):
            if do_norm:
                rstd_from([xT[:, c, gs(g)] for c in range(8)], [[tx[c][g]] for c in range(8)], D, ones_b[:], sq, tsq, rstd, trs)
            for c in range(8):
                if do_norm:
                    o, to = ob[n % 2], tob[n % 2]
                    n += 1
                    mk.op("dve", lambda e, c=c, o=o, g=g: e.scalar_tensor_tensor(out=o[:], in0=xT[:, c, gs(g)], scalar=V("final", 1, c), in1=rstd[:],
                                                                                 op0=ALU.mult, op1=ALU.mult), reads=[tx[c][g], trs, tvec], writes=[to])
                    tdone = mk.tok()
                    mk.dma("sp", yout[c * 128:(c + 1) * 128, gs(g)], o[:], reads=[to], writes=[tdone])
                else:
                    tdone = mk.tok()
                    mk.dma("sp", yout[c * 128:(c + 1) * 128, gs(g)], xT[:, c, gs(g)], reads=[tx[c][g]], writes=[tdone])
                touts.append(tdone)
        mk.finish(touts)
        mk.close()

    need_rope = any(s[0] == "x" and int(s[1]) % 2 == 0 for s in stages)
    if need_rope:
        rope_tables()
    for s in stages:
        if s[0] == "x":
            l = int(s[1])
            if l % 2 == 0:
                even_mixer(l // 2)
            else:
                odd_mixer(l // 2)
        elif s[0] == "m":
            mlp(int(s[1]))
    final_out("f" in stages)
    mk.emit()
    return nc


FULL = ["x0", "m0", "x1", "m1", "x2", "m2", "x3", "m3", "f"]
_CACHE = {}


def kernel(**inputs):
    stages = os.environ.get("MK_STAGES")
    stages = stages.split(",") if stages else FULL
    x = np.asarray(inputs["x"], np.float32)
    pos = np.asarray(inputs["positions"], np.int32)
    sh = prep_shared(inputs)
    key = tuple(stages)
    if not any(st[0] == "m" for st in stages):
        sh["m_w1"] = np.zeros((1, 128, 128), np.float32)
        sh["m_w2"] = np.zeros((1, 128, 128), np.float32)
    if key not in _CACHE:
        _CACHE[key] = build(stages)
    nc = _CACHE[key]
    in_maps = []
    for b in range(8):
        m = dict(sh)
        m["xT"] = np.ascontiguousarray(x[b].T)
        m["pos"] = np.ascontiguousarray(pos[b][None, :])
        in_maps.append(m)
    res = run_bass_kernel_spmd(nc, in_maps, core_ids=list(range(8)))
    out = np.stack([np.asarray(res.results[b]["yT"], np.float32).T for b in range(8)], axis=0)
    return np.ascontiguousarray(out)
```
